# Optimizing a Trainium2 kernel written in Bass

```python
import math
import jax, jax.numpy as jnp
from jax import lax
import numpy as np

D_MODEL = 2048
BATCH = 8
SEQ = 2048
DEPTH = 2

CHUNK = 64
Q_BLOCK = 128
N_MIXERS = 2
ROPE_THETA = 500000.0
ROT_FRACTION = 4
EPS = 1e-6
BRANCH_WIDTH = D_MODEL

DIFF_HEAD_DIM = 128
DIFF_HEADS = BRANCH_WIDTH // (2 * DIFF_HEAD_DIM)
FOX_HEAD_DIM = 128
FOX_HEADS = BRANCH_WIDTH // FOX_HEAD_DIM

N_LAYERS_A = (DEPTH + 1) // 2
N_LAYERS_B = DEPTH // 2

kernel_name = "hybrid_diffattn_fox_gated_trunk"


def rms_norm(x, gain):
    xf = x.astype(jnp.float32)
    y = xf * lax.rsqrt(jnp.mean(xf * xf, axis=-1, keepdims=True) + EPS)
    return (y * gain.astype(jnp.float32)).astype(x.dtype)


def partial_rope(x, positions):
    rot = x.shape[-1] // ROT_FRACTION
    half = rot // 2
    inv_freq = ROPE_THETA ** (-jnp.arange(half, dtype=jnp.float32) / half)
    ang = positions.astype(jnp.float32)[:, :, None, None] * inv_freq
    cos, sin = jnp.cos(ang), jnp.sin(ang)
    xr = x[..., :rot].astype(jnp.float32)
    x1, x2 = xr[..., :half], xr[..., half:]
    xr = jnp.concatenate([x1 * cos - x2 * sin, x2 * cos + x1 * sin], axis=-1)
    return jnp.concatenate([xr.astype(x.dtype), x[..., rot:]], axis=-1)


def chunk_causal_diff_attention(q1, q2, k1, k2, v, lam):
    S = q1.shape[1]
    scale = q1.shape[-1] ** -0.5
    outs = []
    for start in range(0, S, Q_BLOCK):
        end = start + Q_BLOCK
        mask = (jnp.arange(end) // CHUNK)[None, :] <= (jnp.arange(start, end) // CHUNK)[:, None]

        def attn_map(q, k):
            s = jnp.einsum("bqhd,bkhd->bhqk", q[:, start:end].astype(jnp.float32),
                           k[:, :end].astype(jnp.float32)) * scale
            return jax.nn.softmax(jnp.where(mask, s, -jnp.inf), axis=-1)

        p = attn_map(q1, k1) - lam * attn_map(q2, k2)
        outs.append(jnp.einsum("bhqk,bkhe->bqhe", p.astype(v.dtype), v[:, :end]))
    return jnp.concatenate(outs, axis=1)


def forgetting_attention(q, k, v, log_f):
    S = q.shape[1]
    scale = q.shape[-1] ** -0.5
    cum = jnp.transpose(jnp.cumsum(log_f, axis=1), (0, 2, 1))
    outs = []
    for start in range(0, S, Q_BLOCK):
        end = start + Q_BLOCK
        mask = jnp.arange(end)[None, :] <= jnp.arange(start, end)[:, None]
        s = jnp.einsum("bqhd,bkhd->bhqk", q[:, start:end].astype(jnp.float32),
                       k[:, :end].astype(jnp.float32)) * scale
        s = s + (cum[:, :, start:end, None] - cum[:, :, None, :end])
        p = jax.nn.softmax(jnp.where(mask, s, -jnp.inf), axis=-1)
        outs.append(jnp.einsum("bhqk,bkhd->bqhd", p.astype(v.dtype), v[:, :end]))
    return jnp.concatenate(outs, axis=1)


def diff_attn_layer(x, positions, norm_g, w_in, q_norm_g, k_norm_g,
                    lq1, lk1, lq2, lk2, sub_norm_g, w_out, layer_idx):
    B, S, _ = x.shape
    lambda_init = 0.8 - 0.6 * math.exp(-0.3 * layer_idx)
    h = rms_norm(x, norm_g)
    q, k, v, gate = jnp.split(h @ w_in, 4, axis=-1)
    q = partial_rope(rms_norm(q.reshape(B, S, 2 * DIFF_HEADS, DIFF_HEAD_DIM), q_norm_g), positions)
    k = partial_rope(rms_norm(k.reshape(B, S, 2 * DIFF_HEADS, DIFF_HEAD_DIM), k_norm_g), positions)
    q = q.reshape(B, S, DIFF_HEADS, 2, DIFF_HEAD_DIM)
    k = k.reshape(B, S, DIFF_HEADS, 2, DIFF_HEAD_DIM)
    v = v.reshape(B, S, DIFF_HEADS, 2 * DIFF_HEAD_DIM)
    lam = (jnp.exp(jnp.sum(lq1.astype(jnp.float32) * lk1.astype(jnp.float32)))
           - jnp.exp(jnp.sum(lq2.astype(jnp.float32) * lk2.astype(jnp.float32)))
           + lambda_init)
    o = chunk_causal_diff_attention(q[:, :, :, 0], q[:, :, :, 1], k[:, :, :, 0], k[:, :, :, 1], v, lam)
    o = rms_norm(o, sub_norm_g) * (1.0 - lambda_init)
    o = o.reshape(B, S, BRANCH_WIDTH) * jax.nn.silu(gate)
    return x + o @ w_out


def forgetting_attn_layer(x, norm_g, w_in, f_bias, q_norm_g, k_norm_g, w_out):
    B, S, _ = x.shape
    h = rms_norm(x, norm_g)
    proj = h @ w_in
    q, k, v, gate = jnp.split(proj[..., :4 * BRANCH_WIDTH], 4, axis=-1)
    log_f = jax.nn.log_sigmoid(proj[..., 4 * BRANCH_WIDTH:].astype(jnp.float32)
                               + f_bias.astype(jnp.float32))
    q = rms_norm(q.reshape(B, S, FOX_HEADS, FOX_HEAD_DIM), q_norm_g)
    k = rms_norm(k.reshape(B, S, FOX_HEADS, FOX_HEAD_DIM), k_norm_g)
    v = v.reshape(B, S, FOX_HEADS, FOX_HEAD_DIM)
    o = forgetting_attention(q, k, v, log_f).reshape(B, S, BRANCH_WIDTH) * jax.nn.silu(gate)
    return x + o @ w_out


def setup_inputs(seed: int = 0) -> dict:
    key = jax.random.key(seed)
    ks = jax.random.split(key, 20)
    f32 = jnp.float32
    NA, NB = N_LAYERS_A, N_LAYERS_B

    def gain(k, shape):
        return 1.0 + 0.02 * jax.random.normal(k, shape, f32)

    x = jax.random.normal(ks[0], (BATCH, SEQ, D_MODEL), f32)
    offsets = jax.random.randint(ks[1], (BATCH, 1), 0, 64, dtype=jnp.int32) * CHUNK
    positions = (offsets + jnp.arange(SEQ, dtype=jnp.int32)[None, :]).astype(jnp.int32)
    return {
        "x": x,
        "positions": positions,
        "a_norm": gain(ks[2], (NA, D_MODEL)),
        "a_w_in": jax.random.normal(ks[3], (NA, D_MODEL, 4 * BRANCH_WIDTH), f32) * D_MODEL ** -0.5,
        "a_q_norm": gain(ks[4], (NA, DIFF_HEAD_DIM)),
        "a_k_norm": gain(ks[5], (NA, DIFF_HEAD_DIM)),
        "a_lambda_q1": 0.1 * jax.random.normal(ks[6], (NA, DIFF_HEAD_DIM), f32),
        "a_lambda_k1": 0.1 * jax.random.normal(ks[7], (NA, DIFF_HEAD_DIM), f32),
        "a_lambda_q2": 0.1 * jax.random.normal(ks[8], (NA, DIFF_HEAD_DIM), f32),
        "a_lambda_k2": 0.1 * jax.random.normal(ks[9], (NA, DIFF_HEAD_DIM), f32),
        "a_sub_norm": gain(ks[10], (NA, 2 * DIFF_HEAD_DIM)),
        "a_w_out": jax.random.normal(ks[11], (NA, BRANCH_WIDTH, D_MODEL), f32) * BRANCH_WIDTH ** -0.5,
        "b_norm": gain(ks[12], (NB, D_MODEL)),
        "b_w_in": jax.random.normal(ks[13], (NB, D_MODEL, 4 * BRANCH_WIDTH + FOX_HEADS), f32) * D_MODEL ** -0.5,
        "b_f_bias": jax.random.uniform(ks[14], (NB, FOX_HEADS), f32, minval=1.0, maxval=5.0),
        "b_q_norm": gain(ks[15], (NB, FOX_HEAD_DIM)),
        "b_k_norm": gain(ks[16], (NB, FOX_HEAD_DIM)),
        "b_w_out": jax.random.normal(ks[17], (NB, BRANCH_WIDTH, D_MODEL), f32) * BRANCH_WIDTH ** -0.5,
    }


def reference(x, positions, a_norm, a_w_in, a_q_norm, a_k_norm, a_lambda_q1, a_lambda_k1,
              a_lambda_q2, a_lambda_k2, a_sub_norm, a_w_out, b_norm, b_w_in, b_f_bias,
              b_q_norm, b_k_norm, b_w_out):
    for i in range(DEPTH):
        j = i // N_MIXERS
        if i % N_MIXERS == 0:
            x = diff_attn_layer(x, positions, a_norm[j], a_w_in[j], a_q_norm[j], a_k_norm[j],
                                a_lambda_q1[j], a_lambda_k1[j], a_lambda_q2[j], a_lambda_k2[j],
                                a_sub_norm[j], a_w_out[j], layer_idx=i)
        else:
            x = forgetting_attn_layer(x, b_norm[j], b_w_in[j], b_f_bias[j], b_q_norm[j],
                                      b_k_norm[j], b_w_out[j])
    return x
```

```python
import contextlib
import math
import types
import numpy as np
import concourse.bass as bass
import concourse.mybir as mybir
from concourse.bass_utils import run_bass_kernel_spmd

F32 = mybir.dt.float32
BF16 = mybir.dt.bfloat16
I32 = mybir.dt.int32
ALU = mybir.AluOpType
AF = mybir.ActivationFunctionType
AX = mybir.AxisListType

S = 2048
D = 2048
NT = 16
NCH = 16
EPS = 1e-6
NEG = -30000.0
TWO_PI = 2.0 * math.pi
STATS = {}


class Op:
    __slots__ = ("eng", "fn", "deps", "semkey", "count", "is_dma", "needed", "idx", "ninst")

    def __init__(self, eng, fn, is_dma, semkey, idx, ninst):
        self.eng = eng
        self.fn = fn
        self.deps = set()
        self.semkey = semkey
        self.count = None
        self.is_dma = is_dma
        self.needed = False
        self.idx = idx
        self.ninst = ninst


def _freeze(fn):
    if fn.__closure__ is None:
        return fn
    cells = tuple(types.CellType(c.cell_contents) for c in fn.__closure__)
    return types.FunctionType(fn.__code__, fn.__globals__, fn.__name__, fn.__defaults__, cells)


class Prog:
    def __init__(self):
        self.ops = []
        self.lastw = {}
        self.readers = {}
        self.epoch = 0

    def new_epoch(self):
        self.epoch += 1

    def add(self, eng, fn, reads=(), writes=(), dma=None, ninst=1):
        idx = len(self.ops)
        is_dma = dma is not None
        semkey = ("dma", dma) if is_dma else ("eng", eng, self.epoch)
        op = Op(eng, _freeze(fn), is_dma, semkey, idx, ninst)
        ops = self.ops
        deps = op.deps

        def consider(d, raw):
            dop = ops[d]
            if dop.is_dma or is_dma:
                deps.add(d)
            elif dop.eng == eng:
                if eng != "pe" and raw:
                    deps.add(d)
            else:
                deps.add(d)

        lastw = self.lastw
        readers = self.readers
        for r in reads:
            w = lastw.get(r)
            if w is not None:
                consider(w, True)
        for wr in writes:
            w = lastw.get(wr)
            if w is not None:
                consider(w, False)
            rd = readers.get(wr)
            if rd:
                for d in rd.values():
                    consider(d, False)
        rkey = ("d", idx) if is_dma else eng
        for r in reads:
            rd = readers.get(r)
            if rd is None:
                rd = readers[r] = {}
            rd[rkey] = idx
        for wr in writes:
            lastw[wr] = idx
            readers[wr] = {}
        ops.append(op)
        return op

    def emit(self, nc):
        ops = self.ops
        for op in ops:
            for d in op.deps:
                ops[d].needed = True
        counters = {}
        for op in ops:
            if op.is_dma:
                c = counters.get(op.semkey, 0) + 16 * op.ninst
                counters[op.semkey] = c
                op.count = c
            elif op.needed:
                c = counters.get(op.semkey, 0) + 1
                counters[op.semkey] = c
                op.count = c
        semkeys = list(counters.keys())
        self.n_sems = len(semkeys)
        self.max_count = max(counters.values()) if counters else 0
        self.nwaits = 0
        with contextlib.ExitStack() as es:
            sems = {}
            for i, k in enumerate(semkeys):
                sems[k] = es.enter_context(nc.semaphore("s%d" % i))
            block = es.enter_context(nc.Block())
            prog = self

            def run_engine(eng_name, eng):
                waited = {}
                for op in ops:
                    if op.eng != eng_name:
                        continue
                    need = {}
                    for d in op.deps:
                        dop = ops[d]
                        if need.get(dop.semkey, 0) < dop.count:
                            need[dop.semkey] = dop.count
                    for k, v in need.items():
                        if waited.get(k, 0) < v:
                            eng.wait_ge(sems[k], v)
                            waited[k] = v
                            prog.nwaits += 1
                    r = op.fn(eng)
                    if op.is_dma:
                        lst = r if isinstance(r, (list, tuple)) else [r]
                        assert len(lst) == op.ninst, (len(lst), op.ninst)
                        for ins in lst:
                            ins.then_inc(sems[op.semkey], 16)
                    elif op.needed:
                        r.then_inc(sems[op.semkey], 1)
                if eng_name == "sp":
                    for k, v in counters.items():
                        if k[0] == "dma" and waited.get(k, 0) < v:
                            eng.wait_ge(sems[k], v)

            @block.tensor
            def _(e):
                run_engine("pe", e)

            @block.scalar
            def _(e):
                run_engine("act", e)

            @block.vector
            def _(e):
                run_engine("dve", e)

            @block.gpsimd
            def _(e):
                run_engine("pool", e)

            @block.sync
            def _(e):
                run_engine("sp", e)


class V:
    __slots__ = ("ap", "keys")

    def __init__(self, ap, keys):
        self.ap = ap
        self.keys = list(keys)


class Arena:
    GRAN = 1024

    def __init__(self, tensor, name, nbytes):
        self.t = tensor
        self.name = name
        self.nbytes = nbytes
        self.off = 0

    def reset(self, off=0):
        self.off = off

    def alloc(self, shape_free, dtype, parts=128):
        esz = 4 if dtype in (F32, I32) else 2
        n = 1
        for s_ in shape_free:
            n *= s_
        nb = n * esz
        off = (self.off + self.GRAN - 1) // self.GRAN * self.GRAN
        assert off + nb <= self.nbytes, (self.name, off, nb, self.nbytes)
        self.off = off + nb
        ap = self.t[0:parts, off // 2:(off + nb) // 2]
        if esz == 4:
            ap = ap.bitcast(dtype)
        if len(shape_free) == 2:
            ap = ap.rearrange("p (a b) -> p a b", a=shape_free[0])
        elif len(shape_free) == 3:
            ap = ap.rearrange("p (a b c) -> p a b c", a=shape_free[0], b=shape_free[1])
        keys = [(self.name, g) for g in range(off // self.GRAN, (off + nb + self.GRAN - 1) // self.GRAN)]
        return V(ap, keys)


def K(*items):
    out = []
    for it in items:
        if isinstance(it, V):
            out.extend(it.keys)
        elif isinstance(it, list):
            out.extend(it)
        else:
            out.append(it)
    return out


def build(layers, debug=None):
    nc = bass.Bass("TRN2", target_bir_lowering=False)
    dt = nc.dram_tensor
    P = Prog()
    x_d = dt("x", [S, D], F32, kind="ExternalInput").ap()
    out_d = dt("out", [S, D], F32, kind="ExternalOutput").ap()
    cst_d = dt("cst", [128, 416], F32, kind="ExternalInput").ap()
    prm = {}
    if 0 in layers:
        pos_d = dt("pos", [128, 16], I32, kind="ExternalInput").ap()
        prm[0] = dict(
            norm=dt("a_norm", [1, D], F32, kind="ExternalInput").ap(),
            w_in=dt("a_w_in", [D, 8192], F32, kind="ExternalInput").ap(),
            qn=dt("a_q_norm", [1, 128], F32, kind="ExternalInput").ap(),
            kn=dt("a_k_norm", [1, 128], F32, kind="ExternalInput").ap(),
            lam=dt("a_lam", [4, 128], F32, kind="ExternalInput").ap(),
            sub=dt("a_sub_norm", [1, 256], F32, kind="ExternalInput").ap(),
            w_out=dt("a_w_out", [D, D], F32, kind="ExternalInput").ap(),
        )
    if 1 in layers:
        prm[1] = dict(
            norm=dt("b_norm", [1, D], F32, kind="ExternalInput").ap(),
            w_in=dt("b_w_in", [D, 8208], F32, kind="ExternalInput").ap(),
            fb=dt("b_f_bias", [1, 16], F32, kind="ExternalInput").ap(),
            qn=dt("b_q_norm", [1, 128], F32, kind="ExternalInput").ap(),
            kn=dt("b_k_norm", [1, 128], F32, kind="ExternalInput").ap(),
            w_out=dt("b_w_out", [D, D], F32, kind="ExternalInput").ap(),
        )
    dk = "ExternalOutput" if debug else "Internal"
    OT_d = {l: dt("ot%d" % l, [D, S], BF16, kind=dk).ap() for l in layers}
    if len(layers) == 2:
        X1_d = dt("x1s", [S, D], F32, kind="Internal").ap()
    if 1 in layers:
        AUG_d = dt("augs", [6, 16, S], BF16, kind=dk).ap()
    dbg = {}

    with contextlib.ExitStack() as es:
        def sb(name, shape, dtype):
            return es.enter_context(nc.sbuf_tensor(name, shape, dtype))

        def pst(name, shape, dtype):
            return es.enter_context(nc.psum_tensor(name, shape, dtype))

        hT = sb("hT", [128, NCH, S], BF16)
        A1t = sb("A1", [128, 32768], BF16)
        A2_BYTES = 64512
        A2t = sb("A2", [128, A2_BYTES // 2], BF16)
        A2 = Arena(A2t, "A2", A2_BYTES)
        Wslot = [A1t[:, s_ * 8192:(s_ + 1) * 8192].rearrange("p (c n) -> p c n", c=16) for s_ in range(3)]
        qkT = A1t[:, 24576:32768].rearrange("p (h t) -> p h t", h=4)
        Wout = A1t[:, :].rearrange("p (c n) -> p c n", c=16)

        def wkeys(s_):
            return [("A1", s_, i) for i in range(8)]
        QKT_KEY = ("A1", 3)
        A1_ALL = wkeys(0) + wkeys(1) + wkeys(2) + [QKT_KEY]

        psA = [pst("psA%d" % i, [128, 512], F32) for i in range(2)]
        ptrf = [pst("ptr%d" % i, [128, 512], F32) for i in range(2)]
        ptr = [pf[:].bitcast(BF16) for pf in ptrf]
        acc = [pst("acc%d" % i, [128, 512], F32) for i in range(4)]

        cst = sb("cst_sb", [128, 416], F32)
        ident = sb("ident", [128, 128], BF16)
        maskF = sb("maskF", [128, 128], BF16)
        maskD = sb("maskD", [128, 128], BF16)
        cneg = sb("cneg", [128, 4], F32)
        ssq = sb("ssq", [128, 16], F32)
        var = sb("var", [128, 16], F32)
        rstd = sb("rstd", [128, 16], F32)
        ssq4 = sb("ssq4", [128, 4, 4], F32)
        var4 = sb("var4", [128, 4, 4], F32)
        rstd4 = sb("rstd4", [128, 4, 4], F32)
        rinv = sb("rinv", [128, 8], F32)
        nl2 = sb("nl2", [128, 8], F32)
        sso = sb("sso", [128, 8], F32)
        varo = sb("varo", [128, 8], F32)
        rso = sb("rso", [128, 8], F32)
        gqk = {l: sb("gqk%d" % l, [128, 4, 128], F32) for l in layers}

        P.add("sp", lambda e: e.dma_start(out=cst[:], in_=cst_d[:]), writes=["cst"], dma="cst")
        P.add("dve", lambda e: e.tensor_copy(out=ident[:], in_=cst[:, 0:128]), reads=["cst"], writes=["ident"])
        P.add("dve", lambda e: e.tensor_copy(out=maskF[:], in_=cst[:, 128:256]), reads=["cst"], writes=["maskF"])
        P.add("dve", lambda e: e.tensor_copy(out=maskD[:], in_=cst[:, 256:384]), reads=["cst"], writes=["maskD"])
        P.add("pool", lambda e: e.memset(cneg[:], -0.5), writes=["cneg"])

        if 0 in layers:
            posi = sb("posi", [128, 16], I32)
            posf = sb("posf", [128, 16], F32)
            cs_t = sb("cs_t", [128, 16, 32], F32)
            sn_t = sb("sn_t", [128, 16, 32], F32)
            subg = sb("subg", [128, 256], F32)
            nlam = sb("nlam", [128, 1], F32)
            lsum = sb("lsum", [128, 2], F32)
            lexp = sb("lexp", [128, 2], F32)
            ldif = sb("ldif", [128, 1], F32)
            qr = [sb("qr%d" % i, [128, 4, 32], F32) for i in range(2)]
            ra = [sb("ra%d" % i, [128, 4, 32], F32) for i in range(2)]
            rb = [sb("rb%d" % i, [128, 4, 32], F32) for i in range(2)]
            A2.reset()
            ang = A2.alloc([512], F32)
            uu = A2.alloc([512], F32)
            kf = A2.alloc([512], F32)
            rr = A2.alloc([512], F32)
            rc = A2.alloc([512], F32)
            mm = A2.alloc([512], F32)
            ki = A2.alloc([512], I32)
            sin_t = A2.alloc([512], F32)
            lamv = A2.alloc([4, 128], F32)
            lprod = A2.alloc([2, 128], F32)
            P.add("sp", lambda e: e.dma_start(out=posi[:], in_=pos_d[:]), writes=["posi"], dma="posi")
            P.add("dve", lambda e: e.tensor_copy(out=posf[:], in_=posi[:]), reads=["posi"], writes=["posf"])
            posb = bass.AP(posf, 0, [[16, 128], [1, 16], [0, 32]])
            invb = bass.AP(cst, 384, [[416, 128], [0, 16], [1, 32]])
            v3 = lambda v: v.ap.rearrange("p (a b) -> p a b", a=16)
            P.add("dve", lambda e: e.tensor_tensor(out=v3(ang), in0=posb, in1=invb, op=ALU.mult),
                  reads=K("posf", "cst"), writes=K(ang))
            P.add("dve", lambda e: e.tensor_scalar(out=uu.ap, in0=ang.ap, scalar1=1.0 / TWO_PI, scalar2=None, op0=ALU.mult),
                  reads=K(ang), writes=K(uu))
            P.add("dve", lambda e: e.tensor_copy(out=ki.ap, in_=uu.ap), reads=K(uu), writes=K(ki))
            P.add("dve", lambda e: e.tensor_copy(out=kf.ap, in_=ki.ap), reads=K(ki), writes=K(kf))
            C1 = 6.28125
            C2 = TWO_PI - C1
            P.add("dve", lambda e: e.scalar_tensor_tensor(out=rr.ap, in0=kf.ap, scalar=-C1, in1=ang.ap, op0=ALU.mult, op1=ALU.add),
                  reads=K(kf, ang), writes=K(rr))
            P.add("dve", lambda e: e.scalar_tensor_tensor(out=rr.ap, in0=kf.ap, scalar=-C2, in1=rr.ap, op0=ALU.mult, op1=ALU.add),
                  reads=K(kf, rr), writes=K(rr))
            P.add("dve", lambda e: e.tensor_scalar(out=mm.ap, in0=rr.ap, scalar1=math.pi, scalar2=None, op0=ALU.is_gt),
                  reads=K(rr), writes=K(mm))
            P.add("dve", lambda e: e.scalar_tensor_tensor(out=rr.ap, in0=mm.ap, scalar=-TWO_PI, in1=rr.ap, op0=ALU.mult, op1=ALU.add),
                  reads=K(mm, rr), writes=K(rr))
            P.add("dve", lambda e: e.tensor_scalar(out=mm.ap, in0=rr.ap, scalar1=-math.pi, scalar2=None, op0=ALU.is_lt),
                  reads=K(rr), writes=K(mm))
            P.add("dve", lambda e: e.scalar_tensor_tensor(out=rr.ap, in0=mm.ap, scalar=TWO_PI, in1=rr.ap, op0=ALU.mult, op1=ALU.add),
                  reads=K(mm, rr), writes=K(rr))
            P.add("dve", lambda e: e.tensor_scalar(out=rc.ap, in0=rr.ap, scalar1=math.pi / 2, scalar2=None, op0=ALU.add),
                  reads=K(rr), writes=K(rc))
            P.add("dve", lambda e: e.tensor_scalar(out=mm.ap, in0=rc.ap, scalar1=math.pi, scalar2=None, op0=ALU.is_gt),
                  reads=K(rc), writes=K(mm))
            P.add("dve", lambda e: e.scalar_tensor_tensor(out=rc.ap, in0=mm.ap, scalar=-TWO_PI, in1=rc.ap, op0=ALU.mult, op1=ALU.add),
                  reads=K(mm, rc), writes=K(rc))
            PIS = 3.1415925
            P.add("dve", lambda e: e.tensor_scalar(out=rr.ap, in0=rr.ap, scalar1=-PIS, scalar2=PIS, op0=ALU.max, op1=ALU.min),
                  reads=K(rr), writes=K(rr))
            P.add("dve", lambda e: e.tensor_scalar(out=rc.ap, in0=rc.ap, scalar1=-PIS, scalar2=PIS, op0=ALU.max, op1=ALU.min),
                  reads=K(rc), writes=K(rc))
            P.add("act", lambda e: e.activation(out=sin_t.ap, in_=rr.ap, func=AF.Sin), reads=K(rr), writes=K(sin_t))
            P.add("act", lambda e: e.activation(out=cs_t[:].rearrange("p a b -> p (a b)"), in_=rc.ap, func=AF.Sin),
                  reads=K(rc), writes=["cs_t"])
            P.add("dve", lambda e: e.tensor_scalar(out=sn_t[:, :, 0:16], in0=v3(sin_t)[:, :, 0:16], scalar1=-1.0, scalar2=None, op0=ALU.mult),
                  reads=K(sin_t), writes=["sn_t"])
            P.add("dve", lambda e: e.tensor_copy(out=sn_t[:, :, 16:32], in_=v3(sin_t)[:, :, 16:32]),
                  reads=K(sin_t), writes=["sn_t"])
            lam_src = prm[0]["lam"].rearrange("(o r) n -> o r n", o=1).broadcast_to([128, 4, 128])
            P.add("sp", lambda e: e.dma_start(out=lamv.ap, in_=lam_src), writes=K(lamv), dma="lamv")
            lv = lamv.ap.rearrange("p (a b) n -> p a b n", a=2)
            P.add("dve", lambda e: e.tensor_tensor(out=lprod.ap, in0=lv[:, :, 0, :], in1=lv[:, :, 1, :], op=ALU.mult),
                  reads=K(lamv), writes=K(lprod))
            P.add("dve", lambda e: e.tensor_reduce(out=lsum[:], in_=lprod.ap, axis=AX.X, op=ALU.add),
                  reads=K(lprod), writes=["lsum"])
            P.add("act", lambda e: e.activation(out=lexp[:], in_=lsum[:], func=AF.Exp), reads=["lsum"], writes=["lexp"])
            P.add("dve", lambda e: e.tensor_tensor(out=ldif[:], in0=lexp[:, 0:1], in1=lexp[:, 1:2], op=ALU.subtract),
                  reads=["lexp"], writes=["ldif"])
            lambda_init = 0.8 - 0.6 * math.exp(-0.3 * 0)
            P.add("dve", lambda e: e.tensor_scalar(out=nlam[:], in0=ldif[:], scalar1=-1.0, scalar2=-lambda_init, op0=ALU.mult, op1=ALU.add),
                  reads=["ldif"], writes=["nlam"])
            P.add("sp", lambda e: e.dma_start(out=subg[:], in_=prm[0]["sub"].broadcast_to([128, 256])), writes=["subg"], dma="subg")
            P.add("dve", lambda e: e.tensor_scalar(out=subg[:], in0=subg[:], scalar1=0.5 * (1.0 - lambda_init), scalar2=None, op0=ALU.mult),
                  reads=["subg"], writes=["subg"])

        if 1 in layers:
            wf_sb = sb("wf_sb", [128, 16, 16], BF16)
            nfb = sb("nfb", [16, 1], F32)
            wfv = prm[1]["w_in"].rearrange("(c p) n -> p c n", p=128)
            P.add("pool", lambda e, wfv=wfv: e.dma_start(out=wf_sb[:], in_=wfv[:, :, 8192:8208]), writes=["wf_sb"], dma="wf_sb")
            P.add("sp", lambda e: e.dma_start(out=nfb[:], in_=prm[1]["fb"].rearrange("o h -> h o")), writes=["nfb"], dma="nfb")
            P.add("dve", lambda e: e.tensor_scalar(out=nfb[:], in0=nfb[:], scalar1=-1.0, scalar2=None, op0=ALU.mult),
                  reads=["nfb"], writes=["nfb"])

        piece_ctr = [0]
        dmaq = []

        def pump(n=1):
            for _ in range(n):
                if dmaq:
                    dmaq.pop(0)[1]()

        def need_piece(pidx_):
            while dmaq and dmaq[0][0] <= pidx_:
                dmaq.pop(0)[1]()

        def queue_piece(w_in_d, blocks, slot, pidx_):
            wv = w_in_d.rearrange("(c p) n -> p c n", p=128)
            i = 0
            for (c0, w_, d0) in blocks:
                for cq in range(4):
                    def mk(i=i, c0=c0, w_=w_, d0=d0, cq=cq):
                        def rec():
                            P.add("pool", lambda e: e.dma_start(out=Wslot[slot][:, 4 * cq:4 * cq + 4, d0:d0 + w_],
                                                                 in_=wv[:, 4 * cq:4 * cq + 4, c0:c0 + w_]),
                                  writes=[("A1", slot, i)], dma=("w", slot))
                        return rec
                    dmaq.append((pidx_, mk()))
                    i += 1
            assert i == 8

        for li, l in enumerate(layers):
            P.new_epoch()
            pr = prm[l]
            is_fox = (l == 1)
            x_src = x_d if li == 0 else X1_d
            dst = out_d if li == len(layers) - 1 else X1_d

            g = gqk[l]
            qsrc = pr["qn"].rearrange("(o r) n -> o r n", o=1).broadcast_to([128, 2, 128])
            ksrc = pr["kn"].rearrange("(o r) n -> o r n", o=1).broadcast_to([128, 2, 128])
            P.add("sp", lambda e, g=g, qsrc=qsrc, ksrc=ksrc: [e.dma_start(out=g[:, 0:2, :], in_=qsrc), e.dma_start(out=g[:, 2:4, :], in_=ksrc)],
                  writes=[("gqk", l)], dma=("gqk", l), ninst=2)
            P.add("dve", lambda e, g=g: e.tensor_scalar(out=g[:, 0:2, :], in0=g[:, 0:2, :], scalar1=128.0 ** -0.5, scalar2=None, op0=ALU.mult),
                  reads=[("gqk", l)], writes=[("gqk", l)])

            A2.reset()
            xt = [A2.alloc([2048], F32) for _ in range(3)]
            xn = [A2.alloc([2048], BF16) for _ in range(3)]
            gnorm = A2.alloc([2048], F32)
            sqj = A2.alloc([2048], BF16)
            if li == 0:
                P.add("sp", lambda e, pr=pr, gnorm=gnorm: e.dma_start(out=gnorm.ap, in_=pr["norm"].broadcast_to([128, D])),
                      writes=K(gnorm), dma="gnorm")

            def load_x(t, src=x_src, xt=xt):
                s_ = t % 3
                P.add("sp", lambda e: e.dma_start(out=xt[s_].ap, in_=src[t * 128:(t + 1) * 128, :]),
                      reads=[("xsrc", id(src), t)], writes=K(xt[s_]), dma=("xt", s_))

            w_in_d = pr["w_in"]
            pieces = []
            for gi in range(8):
                pieces.append(("qk", gi, [(256 * gi, 256, 0), (2048 + 256 * gi, 256, 256)]))
                pieces.append(("vg", gi, [(4096 + 256 * gi, 256, 0), (6144 + 256 * gi, 256, 256)]))
            for pi in range(3):
                queue_piece(w_in_d, pieces[pi][2], pi % 3, pi)
            next_piece = [3]

            def n_stage_a(t, xt=xt, xn=xn, gnorm=gnorm, sqj=sqj):
                s_ = t % 3
                P.add("act", lambda e: e.activation(out=sqj.ap, in_=xt[s_].ap, func=AF.Square, accum_out=ssq[:, t:t + 1]),
                      reads=K(xt[s_]), writes=K(sqj, ("ssq", t)))
                P.add("dve", lambda e: e.tensor_scalar(out=var[:, t:t + 1], in0=ssq[:, t:t + 1], scalar1=1.0 / D, scalar2=EPS, op0=ALU.mult, op1=ALU.add),
                      reads=[("ssq", t)], writes=[("var", t)])
                P.add("pool", lambda e: e.tensor_tensor(out=rstd[:, t:t + 1], in0=var[:, t:t + 1], in1=cneg[:, 0:1], op=ALU.pow),
                      reads=[("var", t), "cneg"], writes=[("rstd", t)])
                P.add("dve", lambda e: e.scalar_tensor_tensor(out=xn[s_].ap, in0=xt[s_].ap, scalar=rstd[:, t:t + 1], in1=gnorm.ap,
                                                              op0=ALU.mult, op1=ALU.mult),
                      reads=K(xt[s_], ("rstd", t), gnorm), writes=K(xn[s_]))
                if t + 3 < NT:
                    load_x(t + 3)

            def n_stage_b(t, xn=xn):
                s_ = t % 3
                for half in range(2):
                    for j in range(8):
                        c = half * 8 + j
                        P.add("pe", lambda e, c=c, j=j, half=half: e.transpose(out=ptr[half][:, j * 128:(j + 1) * 128],
                                                                                in_=xn[s_].ap[:, c * 128:(c + 1) * 128], identity=ident[:]),
                              reads=K(xn[s_], "ident"), writes=[("ptr", half)])
                    P.add("act", lambda e, half=half: e.activation(out=hT[:, half * 8:half * 8 + 8, t * 128:(t + 1) * 128],
                                                                   in_=ptr[half][:].rearrange("p (a b) -> p a b", a=8), func=AF.Copy),
                          reads=[("ptr", half)], writes=[("hT", t)])

            if li == 0:
                for t in range(3):
                    load_x(t)
                n_stage_a(0)
                n_stage_a(1)
                for t in range(NT):
                    n_stage_b(t)
                    if t + 2 < NT:
                        n_stage_a(t + 2)
                    pump(2)
            while dmaq:
                pump()
            HT_ALL = [("hT", t) for t in range(NT)]

            if is_fox:
                A2.reset()
                ee = A2.alloc([2048], F32, parts=16)
                lf = ee
                ones16 = A2.alloc([2048], F32, parts=16)
                CC = A2.alloc([2048], F32, parts=16)
                r1 = A2.alloc([2048], F32, parts=16)
                hp = [A2.alloc([2048], BF16, parts=16) for _ in range(3)]
                np_ = [A2.alloc([2048], BF16, parts=16) for _ in range(3)]
                P.add("dve", lambda e: e.memset(ones16.ap, 1.0), writes=K(ones16))
                for tg in range(4):
                    for c in range(NCH):
                        P.add("pe", lambda e, tg=tg, c=c: e.matmul(acc[tg][0:16, :], lhsT=wf_sb[:, c, :], rhs=hT[:, c, tg * 512:(tg + 1) * 512],
                                                                   start=(c == 0), stop=(c == NCH - 1)),
                              reads=K("wf_sb", [("hT", t) for t in range(4 * tg, 4 * tg + 4)]), writes=[("acc", tg)])
                    P.add("act", lambda e, tg=tg: e.activation(out=ee.ap[:, tg * 512:(tg + 1) * 512], in_=acc[tg][0:16, :], func=AF.Exp,
                                                               bias=nfb[:, 0:1], scale=-1.0),
                          reads=[("acc", tg), "nfb"], writes=K(ee))
                P.add("act", lambda e: e.activation(out=lf.ap, in_=ee.ap, func=AF.Ln, bias=1.0, scale=1.0), reads=K(ee), writes=K(lf))
                P.add("dve", lambda e: e.tensor_tensor_scan(out=CC.ap, data0=ones16.ap, data1=lf.ap, initial=0.0, op0=ALU.mult, op1=ALU.add),
                      reads=K(ones16, lf), writes=K(CC))
                P.add("dve", lambda e: e.tensor_copy(out=hp[0].ap, in_=CC.ap), reads=K(CC), writes=K(hp[0]))
                P.add("dve", lambda e: e.tensor_tensor(out=r1.ap, in0=CC.ap, in1=hp[0].ap, op=ALU.subtract), reads=K(CC, hp[0]), writes=K(r1))
                P.add("dve", lambda e: e.tensor_copy(out=hp[1].ap, in_=r1.ap), reads=K(r1), writes=K(hp[1]))
                P.add("dve", lambda e: e.tensor_tensor(out=CC.ap, in0=r1.ap, in1=hp[1].ap, op=ALU.subtract), reads=K(r1, hp[1]), writes=K(CC))
                P.add("dve", lambda e: e.tensor_copy(out=hp[2].ap, in_=CC.ap), reads=K(CC), writes=K(hp[2]))
                for i in range(3):
                    P.add("dve", lambda e, i=i: e.tensor_scalar(out=np_[i].ap, in0=hp[i].ap, scalar1=-1.0, scalar2=None, op0=ALU.mult),
                          reads=K(hp[i]), writes=K(np_[i]))
                for i in range(3):
                    P.add("sp", lambda e, i=i: e.dma_start(out=AUG_d[i, :, :], in_=np_[i].ap), reads=K(np_[i]), writes=["AUG_d"], dma=("augw", i))
                    P.add("sp", lambda e, i=i: e.dma_start(out=AUG_d[3 + i, :, :], in_=hp[i].ap), reads=K(hp[i]), writes=["AUG_d"], dma=("augw", 3 + i))

            A2.reset()
            if is_fox:
                v_sb = A2.alloc([16, 2, 130], BF16)
            else:
                v_sb = A2.alloc([16, 260], BF16)
            gate_sb = A2.alloc([16, 256], BF16)
            oTg = [A2.alloc([2, 512], BF16) for _ in range(2)]
            og = [A2.alloc([4, 256], BF16) for _ in range(2)]
            PT = [A2.alloc([512], BF16) for _ in range(4)]
            sqs = [A2.alloc([512], F32) for _ in range(2)]
            sq = sqs[0]
            raw = [A2.alloc([4, 128], F32) for _ in range(3)]
            NQB = 6
            qkb = [A2.alloc([4, 128], BF16) for _ in range(NQB)]
            th = [A2.alloc([256], F32) for _ in range(2)]
            if is_fox:
                augq = [A2.alloc([2048], BF16) for _ in range(2)]
                augk = [A2.alloc([2048], BF16) for _ in range(2)]
            else:
                sg = [A2.alloc([256], F32) for _ in range(2)]
                gcp = [A2.alloc([256], F32) for _ in range(2)]
                t1 = A2.alloc([4, 256], F32)
                otmp = [A2.alloc([256], F32) for _ in range(4)]
            P.add("dve", lambda e, v_sb=v_sb: e.memset(v_sb.ap, 1.0), writes=K(v_sb))
            if is_fox:
                for i in range(2):
                    for av in (augq[i], augk[i]):
                        P.add("dve", lambda e, av=av: e.memset(av.ap, 0.0), writes=K(av))
                        P.add("dve", lambda e, av=av: e.memset(av.ap[0:6, :], 1.0), writes=K(av))

            evac_ctr = [0]
            pt_ctr = [0]
            og_pending = []
            evac2_pending = []
            wov = pr["w_out"].rearrange("(c p) n -> p c n", p=128)

            def load_wout(j, wov=wov):
                wk = wkeys(j) if j < 3 else [QKT_KEY]
                P.add("pool", lambda e: e.dma_start(out=Wout[:, 4 * j:4 * j + 4, :], in_=wov[:, 4 * j:4 * j + 4, :]),
                      writes=wk, dma=("wout", j))
            for gi in range(8):
                pidx = 2 * gi
                slot = pidx % 3
                need_piece(pidx)
                DEFER = 4

                def qk_transposes(t):
                    qb = qkb[t % NQB]
                    s2 = t % 2
                    for j in range(4):
                        P.add("pe", lambda e, j=j: e.matmul(ptrf[s2][:, j * 128:(j + 1) * 128], lhsT=qb.ap[:, j, :], rhs=ident[:], start=True, stop=True),
                              reads=K(qb, "ident"), writes=[("ptr", s2)])
                    P.add("act", lambda e: e.activation(out=qkT[:, :, t * 128:(t + 1) * 128],
                                                        in_=ptrf[s2][:].rearrange("p (a b) -> p a b", a=4), func=AF.Copy),
                          reads=[("ptr", s2)], writes=[QKT_KEY])

                def qk_stage1(t):
                    a = psA[t % 2]
                    akey = ("psA", t % 2)
                    s4 = t % 4
                    rw = raw[t % 3]
                    sqv = sqs[t % 2]
                    P.add("act", lambda e: e.activation(out=sqv.ap, in_=a[:], func=AF.Square), reads=[akey], writes=K(sqv))
                    P.add("act", lambda e: e.activation(out=rw.ap.rearrange("p a b -> p (a b)"), in_=a[:], func=AF.Copy), reads=[akey], writes=K(rw))
                    P.add("dve", lambda e: e.tensor_reduce(out=ssq4[:, s4, :], in_=sqv.ap.rearrange("p (a b) -> p a b", a=4), axis=AX.X, op=ALU.add),
                          reads=K(sqv), writes=[("ssq4", s4)])
                    P.add("dve", lambda e: e.tensor_scalar(out=var4[:, s4, :], in0=ssq4[:, s4, :], scalar1=1.0 / 128, scalar2=EPS, op0=ALU.mult, op1=ALU.add),
                          reads=[("ssq4", s4)], writes=[("var4", s4)])
                    P.add("pool", lambda e: e.tensor_tensor(out=rstd4[:, s4, :], in0=var4[:, s4, :], in1=cneg[:, 0:4], op=ALU.pow),
                          reads=[("var4", s4), "cneg"], writes=[("rstd4", s4)])

                def qk_stage2(t):
                    s2 = t % 2
                    s4 = t % 4
                    rw = raw[t % 3]
                    qb = qkb[t % NQB]
                    if not is_fox:
                        rb32 = bass.AP(rstd4, s4 * 4, [[16, 128], [1, 4], [0, 32]])
                        P.add("pool", lambda e: e.tensor_tensor(out=qr[s2][:], in0=rw.ap[:, :, 0:32], in1=rb32, op=ALU.mult),
                              reads=K(rw, ("rstd4", s4)), writes=[("qr", s2)])
                        P.add("pool", lambda e: e.tensor_tensor(out=qr[s2][:], in0=qr[s2][:], in1=g[:, :, 0:32], op=ALU.mult),
                              reads=[("qr", s2), ("gqk", l)], writes=[("qr", s2)])
                    for h in range(4):
                        P.add("dve", lambda e, h=h: e.scalar_tensor_tensor(out=qb.ap[:, h, :], in0=rw.ap[:, h, :], scalar=rstd4[:, s4, h:h + 1], in1=g[:, h, :],
                                                                           op0=ALU.mult, op1=ALU.mult),
                              reads=K(rw, ("rstd4", s4), ("gqk", l)), writes=K(qb))
                    if not is_fox:
                        csb = bass.AP(cs_t, t * 32, [[512, 128], [0, 4], [1, 32]])
                        snb = bass.AP(sn_t, t * 32, [[512, 128], [0, 4], [16, 2], [1, 16]])
                        qsw = bass.AP(qr[s2], 16, [[128, 128], [32, 4], [-16, 2], [1, 16]])
                        P.add("dve", lambda e: e.tensor_tensor(out=ra[s2][:], in0=qr[s2][:], in1=csb, op=ALU.mult),
                              reads=[("qr", s2), "cs_t"], writes=[("ra", s2)])
                        P.add("dve", lambda e: e.tensor_tensor(out=rb[s2][:].rearrange("p h (a b) -> p h a b", a=2), in0=qsw, in1=snb, op=ALU.mult),
                              reads=[("qr", s2), "sn_t"], writes=[("rb", s2)])
                        P.add("dve", lambda e: e.tensor_tensor(out=qb.ap[:, :, 0:32], in0=ra[s2][:], in1=rb[s2][:], op=ALU.add),
                              reads=[("ra", s2), ("rb", s2)] + K(qb), writes=K(qb))

                for t in range(NT):
                    a = psA[t % 2]
                    akey = ("psA", t % 2)
                    for c in range(NCH):
                        P.add("pe", lambda e, a=a, c=c, t=t, slot=slot: e.matmul(a[:], lhsT=hT[:, c, t * 128:(t + 1) * 128], rhs=Wslot[slot][:, c, :],
                                                                                  start=(c == 0), stop=(c == NCH - 1)),
                              reads=K(("hT", t), wkeys(slot)), writes=[akey])
                    qk_stage1(t)
                    if t >= 1:
                        qk_stage2(t - 1)
                    if t >= DEFER:
                        qk_transposes(t - DEFER)
                    if t == 3 and og_pending:
                        og_pending.pop(0)()
                    if t % 2 == 1:
                        pump(1)
                qk_stage2(NT - 1)
                qk_pending = list(range(NT - DEFER, NT))
                if next_piece[0] < len(pieces):
                    queue_piece(w_in_d, pieces[next_piece[0]][2], next_piece[0] % 3, next_piece[0])
                    next_piece[0] += 1
                pidx = 2 * gi + 1
                slot = pidx % 3
                need_piece(pidx)
                for t in range(NT):
                    a = psA[t % 2]
                    akey = ("psA", t % 2)
                    s2 = t % 2
                    for c in range(NCH):
                        P.add("pe", lambda e, a=a, c=c, t=t, slot=slot: e.matmul(a[:], lhsT=hT[:, c, t * 128:(t + 1) * 128], rhs=Wslot[slot][:, c, :],
                                                                                  start=(c == 0), stop=(c == NCH - 1)),
                              reads=K(("hT", t), wkeys(slot)), writes=[akey])
                    if is_fox:
                        P.add("act", lambda e, a=a, t=t: e.activation(out=v_sb.ap[:, t, :, 0:128], in_=a[:, 0:256].rearrange("p (h d) -> p h d", h=2),
                                                                      func=AF.Copy, scale=0.5),
                              reads=[akey], writes=K(v_sb))
                    else:
                        P.add("act", lambda e, a=a, t=t: e.activation(out=v_sb.ap[:, t, 0:256], in_=a[:, 0:256], func=AF.Copy),
                              reads=[akey], writes=K(v_sb))
                    P.add("act", lambda e, a=a, s2=s2: e.activation(out=th[s2].ap, in_=a[:, 256:512], func=AF.Tanh, scale=0.5),
                          reads=[akey], writes=K(th[s2]))
                    if qk_pending and t >= 4:
                        qk_transposes(qk_pending.pop(0))
                    if is_fox:
                        P.add("dve", lambda e, a=a, s2=s2, t=t: e.scalar_tensor_tensor(out=gate_sb.ap[:, t, :], in0=th[s2].ap, scalar=1.0, in1=a[:, 256:512],
                                                                                         op0=ALU.add, op1=ALU.mult),
                              reads=K(th[s2], akey), writes=K(gate_sb))
                    else:
                        P.add("act", lambda e, a=a, s2=s2: e.activation(out=gcp[s2].ap, in_=a[:, 256:512], func=AF.Copy),
                              reads=[akey], writes=K(gcp[s2]))
                        P.add("dve", lambda e, s2=s2: e.scalar_tensor_tensor(out=sg[s2].ap, in0=th[s2].ap, scalar=1.0, in1=gcp[s2].ap,
                                                                              op0=ALU.add, op1=ALU.mult),
                              reads=K(th[s2], gcp[s2]), writes=K(sg[s2]))
                        P.add("pool", lambda e, s2=s2, t=t: e.tensor_tensor(out=gate_sb.ap[:, t, :], in0=sg[s2].ap, in1=subg[:], op=ALU.mult),
                              reads=K(sg[s2], "subg"), writes=K(gate_sb))
                    if t % 2 == 1:
                        pump(1)
                if next_piece[0] < len(pieces):
                    queue_piece(w_in_d, pieces[next_piece[0]][2], next_piece[0] % 3, next_piece[0])
                    next_piece[0] += 1
                if gi == 7:
                    assert not dmaq
                    for j in range(3):
                        load_wout(j)

                for qg in range(4):
                    ogs = og[qg % 2]
                    for u in range(2):
                        if is_fox:
                            H = 2 * gi + u
                            ab = (2 * gi + u) % 2
                            P.add("sp", lambda e, ab=ab, H=H: e.dma_start(out=augq[ab].ap[0:3, :], in_=AUG_d[0:3, H, :]),
                                  reads=["AUG_d"], writes=K(augq[ab]), dma=("augq", ab)) if qg == 0 else None
                            P.add("sp", lambda e, ab=ab, H=H: e.dma_start(out=augk[ab].ap[3:6, :], in_=AUG_d[3:6, H, :]),
                                  reads=["AUG_d"], writes=K(augk[ab]), dma=("augk", ab)) if qg == 0 else None
                            Nv = 129
                            vv = lambda kt, u=u, v_sb=v_sb: v_sb.ap[:, kt, u, 0:129]
                        else:
                            Nv = 257
                            vv = lambda kt, v_sb=v_sb: v_sb.ap[:, kt, 0:257]
                        steps = list(range(4 * qg + 4))

                        def rec_S(kt, u=u, qg=qg):
                            q0 = max(4 * qg, kt) * 128
                            n = (4 * qg + 4) * 128 - q0
                            b = psA[kt % 2]
                            bkey = ("psA", kt % 2)
                            diag = kt >= 4 * qg
                            last_plain = (not is_fox) and (not diag)
                            P.add("pe", lambda e: e.matmul(b[:, 0:n], lhsT=qkT[:, 2 + u, kt * 128:(kt + 1) * 128], rhs=qkT[:, u, q0:q0 + n],
                                                           start=True, stop=last_plain),
                                  reads=[QKT_KEY], writes=[bkey])
                            if is_fox:
                                P.add("pe", lambda e: e.matmul(b[:, 0:n], lhsT=augk[ab].ap[:, kt * 128:(kt + 1) * 128], rhs=augq[ab].ap[:, q0:q0 + n],
                                                               start=False, stop=(not diag)),
                                      reads=K(augk[ab], augq[ab]), writes=[bkey])
                            if diag:
                                mk_ = maskF if is_fox else maskD
                                mkey = "maskF" if is_fox else "maskD"
                                P.add("pe", lambda e: e.matmul(b[:, 0:128], lhsT=ident[:], rhs=mk_[:], start=False, stop=True),
                                      reads=["ident", mkey], writes=[bkey])
                            return (b, bkey, q0, n)

                        def rec_rest(kt, info, u=u, qg=qg):
                            b, bkey, q0, n = info
                            pslot = pt_ctr[0] % 4
                            pt_ctr[0] += 1
                            ptile = PT[pslot]
                            P.add("act", lambda e: e.activation(out=ptile.ap[:, 0:n], in_=b[:, 0:n], func=AF.Exp),
                                  reads=[bkey], writes=K(ptile))
                            for qt in range(max(4 * qg, kt), 4 * qg + 4):
                                j = qt - 4 * qg
                                off = qt * 128 - q0
                                P.add("pe", lambda e, j=j, off=off, qt=qt: e.matmul(acc[j][:, 0:Nv], lhsT=ptile.ap[:, off:off + 128], rhs=vv(kt),
                                                                                     start=(kt == 0), stop=(kt == qt)),
                                      reads=K(ptile, v_sb), writes=[("acc", j)])

                        info = rec_S(steps[0])
                        for si, kt in enumerate(steps):
                            nxt = rec_S(steps[si + 1]) if si + 1 < len(steps) else None
                            rec_rest(kt, info)
                            info = nxt

                        while evac2_pending:
                            evac2_pending.pop(0)()
                        esb = (evac_ctr[0] % 2) * 4
                        evac_ctr[0] += 1
                        for j in range(4):
                            qt = 4 * qg + j
                            es_ = esb + j
                            dcol = 128 if is_fox else 256
                            P.add("dve", lambda e, j=j, es_=es_, dcol=dcol: e.reciprocal(out=rinv[:, es_:es_ + 1], in_=acc[j][:, dcol:dcol + 1]),
                                  reads=[("acc", j)], writes=[("rinv", es_)])
                            if is_fox:
                                P.add("dve", lambda e, j=j, es_=es_, qt=qt, u=u, ogs=ogs: e.scalar_tensor_tensor(
                                    out=ogs.ap[:, j, u * 128:(u + 1) * 128], in0=acc[j][:, 0:128], scalar=rinv[:, es_:es_ + 1],
                                    in1=gate_sb.ap[:, qt, u * 128:(u + 1) * 128], op0=ALU.mult, op1=ALU.mult),
                                    reads=K(("acc", j), ("rinv", es_), gate_sb), writes=K(ogs))
                            elif u == 0:
                                P.add("dve", lambda e, j=j, es_=es_: e.tensor_scalar(out=t1.ap[:, j, :], in0=acc[j][:, 0:256], scalar1=rinv[:, es_:es_ + 1],
                                                                                      scalar2=None, op0=ALU.mult),
                                      reads=[("acc", j), ("rinv", es_)], writes=K(t1))
                            else:
                                o2 = otmp[j]
                                P.add("dve", lambda e, es_=es_: e.tensor_tensor(out=nl2[:, es_:es_ + 1], in0=rinv[:, es_:es_ + 1], in1=nlam[:], op=ALU.mult),
                                      reads=[("rinv", es_), "nlam"], writes=[("nl2", es_)])
                                P.add("dve", lambda e, j=j, es_=es_, o2=o2: e.scalar_tensor_tensor(out=o2.ap, in0=acc[j][:, 0:256], scalar=nl2[:, es_:es_ + 1],
                                                                                                     in1=t1.ap[:, j, :], op0=ALU.mult, op1=ALU.add),
                                      reads=K(("acc", j), ("nl2", es_), t1), writes=K(o2))
                        if (not is_fox) and u == 1:
                            for j in range(4):
                                es_ = esb + j
                                o2 = otmp[j]
                                jk = th[j % 2]
                                P.add("act", lambda e, es_=es_, o2=o2, jk=jk: e.activation(out=jk.ap, in_=o2.ap, func=AF.Square, accum_out=sso[:, es_:es_ + 1]),
                                      reads=K(o2), writes=K(jk, ("sso", es_)))

                            def evac2(esb=esb, qg=qg, ogs=ogs):
                                P.add("dve", lambda e: e.tensor_scalar(out=varo[:, esb:esb + 4], in0=sso[:, esb:esb + 4], scalar1=1.0 / 256, scalar2=EPS,
                                                                       op0=ALU.mult, op1=ALU.add),
                                      reads=[("sso", esb + j) for j in range(4)], writes=[("varo", esb + j) for j in range(4)])
                                P.add("pool", lambda e: e.tensor_tensor(out=rso[:, esb:esb + 4], in0=varo[:, esb:esb + 4], in1=cneg[:, 0:4], op=ALU.pow),
                                      reads=[("varo", esb + j) for j in range(4)] + ["cneg"], writes=[("rso", esb + j) for j in range(4)])
                                for j in range(4):
                                    qt = 4 * qg + j
                                    es_ = esb + j
                                    o2 = otmp[j]
                                    P.add("dve", lambda e, j=j, es_=es_, o2=o2, qt=qt: e.scalar_tensor_tensor(
                                        out=ogs.ap[:, j, :], in0=o2.ap, scalar=rso[:, es_:es_ + 1], in1=gate_sb.ap[:, qt, :], op0=ALU.mult, op1=ALU.mult),
                                        reads=K(o2, ("rso", es_), gate_sb), writes=K(ogs))
                            evac2_pending.append(evac2)
                    def og_tr(qg=qg, ogs=ogs, gi=gi, l=l, oTg=oTg):
                        tp = ptr[qg % 2]
                        for c in range(2):
                            for j in range(4):
                                P.add("pe", lambda e, c=c, j=j: e.transpose(out=tp[:, (c * 4 + j) * 128:(c * 4 + j + 1) * 128],
                                                                            in_=ogs.ap[:, j, c * 128:(c + 1) * 128], identity=ident[:]),
                                      reads=K(ogs, "ident"), writes=[("ptr", qg % 2)])
                        ot_ = oTg[qg % 2]
                        P.add("act", lambda e: e.activation(out=ot_.ap, in_=tp[:].rearrange("p (c n) -> p c n", c=2), func=AF.Copy),
                              reads=[("ptr", qg % 2)], writes=K(ot_))
                        P.add("sp", lambda e: e.dma_start(out=OT_d[l][gi * 256:(gi + 1) * 256, qg * 512:(qg + 1) * 512].rearrange("(c p) t -> p c t", p=128),
                                                          in_=ot_.ap),
                              reads=K(ot_), writes=[("OT", l, qg)], dma=("oTg", qg % 2))
                    if og_pending:
                        og_pending.pop(0)()
                    og_pending.append(og_tr)
                    pump(1)
                while evac2_pending:
                    evac2_pending.pop(0)()
            while og_pending:
                og_pending.pop(0)()
            while dmaq:
                pump()

            load_wout(3)
            fuse_next = (li + 1 < len(layers))
            A2.reset()
            xt = [A2.alloc([2048], F32) for _ in range(3)]
            oTin = [A2.alloc([16, 256], BF16) for _ in range(2)]
            if fuse_next:
                xn = [A2.alloc([2048], BF16) for _ in range(2)]
                gnorm = A2.alloc([2048], F32)
                sqj = A2.alloc([2048], BF16)
                nprm = prm[layers[li + 1]]
                P.add("sp", lambda e, nprm=nprm, gnorm=gnorm: e.dma_start(out=gnorm.ap, in_=nprm["norm"].broadcast_to([128, D])),
                      writes=K(gnorm), dma="gnorm")

                def o_stage_a(t, xt=xt, xn=xn, gnorm=gnorm, sqj=sqj):
                    s_ = t % 3
                    s2 = t % 2
                    P.add("act", lambda e: e.activation(out=sqj.ap, in_=xt[s_].ap, func=AF.Square, accum_out=ssq[:, t:t + 1]),
                          reads=K(xt[s_]), writes=K(sqj, ("ssq", t)))
                    P.add("dve", lambda e: e.tensor_scalar(out=var[:, t:t + 1], in0=ssq[:, t:t + 1], scalar1=1.0 / D, scalar2=EPS, op0=ALU.mult, op1=ALU.add),
                          reads=[("ssq", t)], writes=[("var", t)])
                    P.add("pool", lambda e: e.tensor_tensor(out=rstd[:, t:t + 1], in0=var[:, t:t + 1], in1=cneg[:, 0:1], op=ALU.pow),
                          reads=[("var", t), "cneg"], writes=[("rstd", t)])
                    P.add("dve", lambda e: e.scalar_tensor_tensor(out=xn[s2].ap, in0=xt[s_].ap, scalar=rstd[:, t:t + 1], in1=gnorm.ap,
                                                                  op0=ALU.mult, op1=ALU.mult),
                          reads=K(xt[s_], ("rstd", t), gnorm), writes=K(xn[s2]))

                def o_stage_b(t, xn=xn):
                    s2 = t % 2
                    for half in range(2):
                        for j in range(8):
                            c = half * 8 + j
                            P.add("pe", lambda e, c=c, j=j, half=half: e.transpose(out=ptr[half][:, j * 128:(j + 1) * 128],
                                                                                    in_=xn[s2].ap[:, c * 128:(c + 1) * 128], identity=ident[:]),
                                  reads=K(xn[s2], "ident"), writes=[("ptr", half)])
                        P.add("act", lambda e, half=half: e.activation(out=hT[:, half * 8:half * 8 + 8, t * 128:(t + 1) * 128],
                                                                       in_=ptr[half][:].rearrange("p (a b) -> p a b", a=8), func=AF.Copy),
                              reads=[("ptr", half)], writes=[("hT", t)])
            otv = OT_d[l].rearrange("(c p) t -> p c t", p=128)

            def load_o(tq, oTin=oTin, otv=otv):
                s_ = tq % 2
                P.add("sp", lambda e: [e.dma_start(out=oTin[s_].ap[:, 8 * i:8 * i + 8, :], in_=otv[:, 8 * i:8 * i + 8, tq * 256:(tq + 1) * 256]) for i in range(2)],
                      reads=[("OT", l, tq // 2)], writes=K(oTin[s_]), dma=("oTin", s_), ninst=2)

            def load_xr(t, src=x_src, xt=xt):
                s_ = t % 3
                P.add("sp", lambda e: e.dma_start(out=xt[s_].ap, in_=src[t * 128:(t + 1) * 128, :]),
                      reads=[("xsrc", id(src), t)], writes=K(xt[s_]), dma=("xt", s_))

            load_o(0)
            load_xr(0)
            load_xr(1)
            load_o(1)
            for t in range(NT):
                tq, tt = divmod(t, 2)
                s_ = t % 3
                so = tq % 2
                for n in range(4):
                    y = psA[n % 2]
                    ykey = ("psA", n % 2)
                    for c in range(NCH):
                        P.add("pe", lambda e, y=y, c=c, so=so, tt=tt, n=n: e.matmul(y[:], lhsT=oTin[so].ap[:, c, tt * 128:(tt + 1) * 128],
                                                                                    rhs=Wout[:, c, n * 512:(n + 1) * 512], start=(c == 0), stop=(c == NCH - 1)),
                              reads=K(oTin[so], A1_ALL), writes=[ykey])
                    P.add("dve", lambda e, y=y, s_=s_, n=n: e.tensor_tensor(out=xt[s_].ap[:, n * 512:(n + 1) * 512], in0=y[:],
                                                                            in1=xt[s_].ap[:, n * 512:(n + 1) * 512], op=ALU.add),
                          reads=K(ykey, xt[s_]), writes=K(xt[s_]))
                if t + 2 < NT:
                    load_xr(t + 2)
                if tt == 1 and tq + 2 < 8:
                    load_o(tq + 2)
                P.add("sp", lambda e, t=t, s_=s_, dst=dst: e.dma_start(out=dst[t * 128:(t + 1) * 128, :], in_=xt[s_].ap),
                      reads=K(xt[s_]), writes=[("xsrc", id(dst), t)], dma=("xst", s_))
                if fuse_next:
                    o_stage_a(t)
                    if t >= 1:
                        o_stage_b(t - 1)
            if fuse_next:
                o_stage_b(NT - 1)

        P.emit(nc)
    STATS[tuple(layers)] = dict(nops=len(P.ops), nsems=P.n_sems, maxcount=P.max_count, nwaits=P.nwaits)
    return nc


def _consts():
    c = np.zeros((128, 416), np.float32)
    c[:, 0:128] = np.eye(128, dtype=np.float32)
    k = np.arange(128)[:, None]
    q = np.arange(128)[None, :]
    c[:, 128:256] = np.where(k > q, NEG, 0.0)
    c[:, 256:384] = np.where((k // 64) > (q // 64), NEG, 0.0)
    half = 16
    inv = np.float32(500000.0) ** (-(np.arange(half, dtype=np.float32) / np.float32(half)))
    c[:, 384:400] = inv[None, :]
    c[:, 400:416] = inv[None, :]
    return c


_NC_CACHE = {}


def _get_nc(layers):
    if layers not in _NC_CACHE:
        _NC_CACHE[layers] = build(layers)
    return _NC_CACHE[layers]


def _maps(layers, xs, positions, prm):
    cst = _consts()
    maps = []
    for b in range(8):
        m = {"x": np.ascontiguousarray(xs[b]), "cst": cst}
        if 0 in layers:
            m["pos"] = np.ascontiguousarray(positions[b].reshape(16, 128).T.astype(np.int32))
            m["a_norm"] = prm["a_norm"]
            m["a_w_in"] = prm["a_w_in"]
            m["a_q_norm"] = prm["a_q_norm"]
            m["a_k_norm"] = prm["a_k_norm"]
            m["a_lam"] = prm["a_lam"]
            m["a_sub_norm"] = prm["a_sub_norm"]
            m["a_w_out"] = prm["a_w_out"]
        if 1 in layers:
            m["b_norm"] = prm["b_norm"]
            m["b_w_in"] = prm["b_w_in"]
            m["b_f_bias"] = prm["b_f_bias"]
            m["b_q_norm"] = prm["b_q_norm"]
            m["b_k_norm"] = prm["b_k_norm"]
            m["b_w_out"] = prm["b_w_out"]
        maps.append(m)
    return maps


FUSED = True


def kernel(x, positions, a_norm, a_w_in, a_q_norm, a_k_norm, a_lambda_q1, a_lambda_k1,
           a_lambda_q2, a_lambda_k2, a_sub_norm, a_w_out, b_norm, b_w_in, b_f_bias,
           b_q_norm, b_k_norm, b_w_out):
    f = lambda a: np.ascontiguousarray(np.asarray(a, dtype=np.float32))
    prm = dict(
        a_norm=f(a_norm).reshape(1, D), a_w_in=f(a_w_in)[0], a_q_norm=f(a_q_norm).reshape(1, 128),
        a_k_norm=f(a_k_norm).reshape(1, 128),
        a_lam=np.ascontiguousarray(np.concatenate([f(a_lambda_q1).reshape(1, 128), f(a_lambda_k1).reshape(1, 128),
                                                   f(a_lambda_q2).reshape(1, 128), f(a_lambda_k2).reshape(1, 128)], axis=0)),
        a_sub_norm=f(a_sub_norm).reshape(1, 256), a_w_out=f(a_w_out)[0],
        b_norm=f(b_norm).reshape(1, D), b_w_in=f(b_w_in)[0], b_f_bias=f(b_f_bias).reshape(1, 16),
        b_q_norm=f(b_q_norm).reshape(1, 128), b_k_norm=f(b_k_norm).reshape(1, 128), b_w_out=f(b_w_out)[0],
    )
    x = f(x)
    positions = np.asarray(positions)
    cores = list(range(8))
    if FUSED:
        nc = _get_nc((0, 1))
        res = run_bass_kernel_spmd(nc, _maps((0, 1), x, positions, prm), core_ids=cores)
        return np.stack([r["out"] for r in res.results], axis=0)
    nc0 = _get_nc((0,))
    res0 = run_bass_kernel_spmd(nc0, _maps((0,), x, positions, prm), core_ids=cores)
    x1 = [r["out"] for r in res0.results]
    nc1 = _get_nc((1,))
    res1 = run_bass_kernel_spmd(nc1, _maps((1,), x1, positions, prm), core_ids=cores)
    return np.stack([r["out"] for r in res1.results], axis=0)
```

```python
import contextlib
import math
import types
import numpy as np
import concourse.bass as bass
import concourse.mybir as mybir
from concourse.bass_utils import run_bass_kernel_spmd

F32 = mybir.dt.float32
BF16 = mybir.dt.bfloat16
I32 = mybir.dt.int32
ALU = mybir.AluOpType
AF = mybir.ActivationFunctionType
AX = mybir.AxisListType

S = 2048
D = 2048
NT = 16
NCH = 16
EPS = 1e-6
NEG = -30000.0
TWO_PI = 2.0 * math.pi
STATS = {}


class Op:
    __slots__ = ("eng", "fn", "deps", "semkey", "count", "is_dma", "needed", "idx", "ninst")

    def __init__(self, eng, fn, is_dma, semkey, idx, ninst):
        self.eng = eng
        self.fn = fn
        self.deps = set()
        self.semkey = semkey
        self.count = None
        self.is_dma = is_dma
        self.needed = False
        self.idx = idx
        self.ninst = ninst


def _freeze(fn):
    if fn.__closure__ is None:
        return fn
    cells = tuple(types.CellType(c.cell_contents) for c in fn.__closure__)
    return types.FunctionType(fn.__code__, fn.__globals__, fn.__name__, fn.__defaults__, cells)


class Prog:
    def __init__(self):
        self.ops = []
        self.lastw = {}
        self.readers = {}
        self.epoch = 0

    def new_epoch(self):
        self.epoch += 1

    def add(self, eng, fn, reads=(), writes=(), dma=None, ninst=1):
        idx = len(self.ops)
        is_dma = dma is not None
        semkey = ("dma", dma) if is_dma else ("eng", eng, self.epoch)
        op = Op(eng, _freeze(fn), is_dma, semkey, idx, ninst)
        ops = self.ops
        deps = op.deps

        def consider(d, raw):
            dop = ops[d]
            if dop.is_dma or is_dma:
                deps.add(d)
            elif dop.eng == eng:
                if eng != "pe" and raw:
                    deps.add(d)
            else:
                deps.add(d)

        lastw = self.lastw
        readers = self.readers
        for r in reads:
            w = lastw.get(r)
            if w is not None:
                consider(w, True)
        for wr in writes:
            w = lastw.get(wr)
            if w is not None:
                consider(w, False)
            rd = readers.get(wr)
            if rd:
                for d in rd.values():
                    consider(d, False)
        rkey = ("d", idx) if is_dma else eng
        for r in reads:
            rd = readers.get(r)
            if rd is None:
                rd = readers[r] = {}
            rd[rkey] = idx
        for wr in writes:
            lastw[wr] = idx
            readers[wr] = {}
        ops.append(op)
        return op

    def emit(self, nc):
        ops = self.ops
        for op in ops:
            for d in op.deps:
                ops[d].needed = True
        counters = {}
        for op in ops:
            if op.is_dma:
                c = counters.get(op.semkey, 0) + 16 * op.ninst
                counters[op.semkey] = c
                op.count = c
            elif op.needed:
                c = counters.get(op.semkey, 0) + 1
                counters[op.semkey] = c
                op.count = c
        semkeys = list(counters.keys())
        self.n_sems = len(semkeys)
        self.max_count = max(counters.values()) if counters else 0
        self.nwaits = 0
        with contextlib.ExitStack() as es:
            sems = {}
            for i, k in enumerate(semkeys):
                sems[k] = es.enter_context(nc.semaphore("s%d" % i))
            block = es.enter_context(nc.Block())
            prog = self

            def run_engine(eng_name, eng):
                waited = {}
                for op in ops:
                    if op.eng != eng_name:
                        continue
                    need = {}
                    for d in op.deps:
                        dop = ops[d]
                        if need.get(dop.semkey, 0) < dop.count:
                            need[dop.semkey] = dop.count
                    for k, v in need.items():
                        if waited.get(k, 0) < v:
                            eng.wait_ge(sems[k], v)
                            waited[k] = v
                            prog.nwaits += 1
                    r = op.fn(eng)
                    if op.is_dma:
                        lst = r if isinstance(r, (list, tuple)) else [r]
                        assert len(lst) == op.ninst, (len(lst), op.ninst)
                        for ins in lst:
                            ins.then_inc(sems[op.semkey], 16)
                    elif op.needed:
                        r.then_inc(sems[op.semkey], 1)
                if eng_name == "sp":
                    for k, v in counters.items():
                        if k[0] == "dma" and waited.get(k, 0) < v:
                            eng.wait_ge(sems[k], v)

            @block.tensor
            def _(e):
                run_engine("pe", e)

            @block.scalar
            def _(e):
                run_engine("act", e)

            @block.vector
            def _(e):
                run_engine("dve", e)

            @block.gpsimd
            def _(e):
                run_engine("pool", e)

            @block.sync
            def _(e):
                run_engine("sp", e)


class V:
    __slots__ = ("ap", "keys")

    def __init__(self, ap, keys):
        self.ap = ap
        self.keys = list(keys)


class Arena:
    GRAN = 1024

    def __init__(self, tensor, name, nbytes):
        self.t = tensor
        self.name = name
        self.nbytes = nbytes
        self.off = 0

    def reset(self, off=0):
        self.off = off

    def alloc(self, shape_free, dtype, parts=128):
        esz = 4 if dtype in (F32, I32) else 2
        n = 1
        for s_ in shape_free:
            n *= s_
        nb = n * esz
        off = (self.off + self.GRAN - 1) // self.GRAN * self.GRAN
        assert off + nb <= self.nbytes, (self.name, off, nb, self.nbytes)
        self.off = off + nb
        ap = self.t[0:parts, off // 2:(off + nb) // 2]
        if esz == 4:
            ap = ap.bitcast(dtype)
        if len(shape_free) == 2:
            ap = ap.rearrange("p (a b) -> p a b", a=shape_free[0])
        elif len(shape_free) == 3:
            ap = ap.rearrange("p (a b c) -> p a b c", a=shape_free[0], b=shape_free[1])
        keys = [(self.name, g) for g in range(off // self.GRAN, (off + nb + self.GRAN - 1) // self.GRAN)]
        return V(ap, keys)


def K(*items):
    out = []
    for it in items:
        if isinstance(it, V):
            out.extend(it.keys)
        elif isinstance(it, list):
            out.extend(it)
        else:
            out.append(it)
    return out


def build(layers, debug=None):
    nc = bass.Bass("TRN2", target_bir_lowering=False)
    dt = nc.dram_tensor
    P = Prog()
    x_d = dt("x", [S, D], F32, kind="ExternalInput").ap()
    out_d = dt("out", [S, D], F32, kind="ExternalOutput").ap()
    cst_d = dt("cst", [128, 416], F32, kind="ExternalInput").ap()
    prm = {}
    if 0 in layers:
        pos_d = dt("pos", [128, 16], I32, kind="ExternalInput").ap()
        prm[0] = dict(
            norm=dt("a_norm", [1, D], F32, kind="ExternalInput").ap(),
            w_in=dt("a_w_in", [D, 8192], F32, kind="ExternalInput").ap(),
            qn=dt("a_q_norm", [1, 128], F32, kind="ExternalInput").ap(),
            kn=dt("a_k_norm", [1, 128], F32, kind="ExternalInput").ap(),
            lam=dt("a_lam", [4, 128], F32, kind="ExternalInput").ap(),
            sub=dt("a_sub_norm", [1, 256], F32, kind="ExternalInput").ap(),
            w_out=dt("a_w_out", [D, D], F32, kind="ExternalInput").ap(),
        )
    if 1 in layers:
        prm[1] = dict(
            norm=dt("b_norm", [1, D], F32, kind="ExternalInput").ap(),
            w_in=dt("b_w_in", [D, 8208], F32, kind="ExternalInput").ap(),
            fb=dt("b_f_bias", [1, 16], F32, kind="ExternalInput").ap(),
            qn=dt("b_q_norm", [1, 128], F32, kind="ExternalInput").ap(),
            kn=dt("b_k_norm", [1, 128], F32, kind="ExternalInput").ap(),
            w_out=dt("b_w_out", [D, D], F32, kind="ExternalInput").ap(),
        )
    dk = "ExternalOutput" if debug else "Internal"
    OT_d = {l: dt("ot%d" % l, [D, S], BF16, kind=dk).ap() for l in layers}
    if len(layers) == 2:
        X1_d = dt("x1s", [S, D], F32, kind="Internal").ap()
    if 1 in layers:
        AUG_d = dt("augs", [6, 16, S], BF16, kind=dk).ap()
    dbg = {}

    with contextlib.ExitStack() as es:
        def sb(name, shape, dtype):
            return es.enter_context(nc.sbuf_tensor(name, shape, dtype))

        def pst(name, shape, dtype):
            return es.enter_context(nc.psum_tensor(name, shape, dtype))

        hT = sb("hT", [128, NCH, S], BF16)
        A1t = sb("A1", [128, 32768], BF16)
        A2_BYTES = 64512
        A2t = sb("A2", [128, A2_BYTES // 2], BF16)
        A2 = Arena(A2t, "A2", A2_BYTES)
        Wslot = [A1t[:, s_ * 8192:(s_ + 1) * 8192].rearrange("p (c n) -> p c n", c=16) for s_ in range(3)]
        qkT = A1t[:, 24576:32768].rearrange("p (h t) -> p h t", h=4)
        Wout = A1t[:, :].rearrange("p (c n) -> p c n", c=16)

        def wkeys(s_):
            return [("A1", s_, i) for i in range(8)]
        QKT_KEY = ("A1", 3)
        A1_ALL = wkeys(0) + wkeys(1) + wkeys(2) + [QKT_KEY]

        psA = [pst("psA%d" % i, [128, 512], F32) for i in range(2)]
        ptrf = [pst("ptr%d" % i, [128, 512], F32) for i in range(2)]
        ptr = [pf[:].bitcast(BF16) for pf in ptrf]
        acc = [pst("acc%d" % i, [128, 512], F32) for i in range(4)]

        cst = sb("cst_sb", [128, 416], F32)
        ident = sb("ident", [128, 128], BF16)
        maskF = sb("maskF", [128, 128], BF16)
        maskD = sb("maskD", [128, 128], BF16)
        cneg = sb("cneg", [128, 4], F32)
        ssq = sb("ssq", [128, 16], F32)
        var = sb("var", [128, 16], F32)
        rstd = sb("rstd", [128, 16], F32)
        ssq4 = sb("ssq4", [128, 4, 4], F32)
        var4 = sb("var4", [128, 4, 4], F32)
        rstd4 = sb("rstd4", [128, 4, 4], F32)
        rinv = sb("rinv", [128, 8], F32)
        nl2 = sb("nl2", [128, 8], F32)
        sso = sb("sso", [128, 8], F32)
        varo = sb("varo", [128, 8], F32)
        rso = sb("rso", [128, 8], F32)
        gqk = {l: sb("gqk%d" % l, [128, 4, 128], F32) for l in layers}

        P.add("sp", lambda e: e.dma_start(out=cst[:], in_=cst_d[:]), writes=["cst"], dma="cst")
        P.add("dve", lambda e: e.tensor_copy(out=ident[:], in_=cst[:, 0:128]), reads=["cst"], writes=["ident"])
        P.add("dve", lambda e: e.tensor_copy(out=maskF[:], in_=cst[:, 128:256]), reads=["cst"], writes=["maskF"])
        P.add("dve", lambda e: e.tensor_copy(out=maskD[:], in_=cst[:, 256:384]), reads=["cst"], writes=["maskD"])
        P.add("pool", lambda e: e.memset(cneg[:], -0.5), writes=["cneg"])

        if 0 in layers:
            posi = sb("posi", [128, 16], I32)
            posf = sb("posf", [128, 16], F32)
            cs_t = sb("cs_t", [128, 16, 32], F32)
            sn_t = sb("sn_t", [128, 16, 32], F32)
            subg = sb("subg", [128, 256], F32)
            nlam = sb("nlam", [128, 1], F32)
            lsum = sb("lsum", [128, 2], F32)
            lexp = sb("lexp", [128, 2], F32)
            ldif = sb("ldif", [128, 1], F32)
            qr = [sb("qr%d" % i, [128, 4, 32], F32) for i in range(2)]
            ra = [sb("ra%d" % i, [128, 4, 32], F32) for i in range(2)]
            rb = [sb("rb%d" % i, [128, 4, 32], F32) for i in range(2)]
            A2.reset()
            ang = A2.alloc([512], F32)
            uu = A2.alloc([512], F32)
            kf = A2.alloc([512], F32)
            rr = A2.alloc([512], F32)
            rc = A2.alloc([512], F32)
            mm = A2.alloc([512], F32)
            ki = A2.alloc([512], I32)
            sin_t = A2.alloc([512], F32)
            lamv = A2.alloc([4, 128], F32)
            lprod = A2.alloc([2, 128], F32)
            P.add("sp", lambda e: e.dma_start(out=posi[:], in_=pos_d[:]), writes=["posi"], dma="posi")
            P.add("dve", lambda e: e.tensor_copy(out=posf[:], in_=posi[:]), reads=["posi"], writes=["posf"])
            posb = bass.AP(posf, 0, [[16, 128], [1, 16], [0, 32]])
            invb = bass.AP(cst, 384, [[416, 128], [0, 16], [1, 32]])
            v3 = lambda v: v.ap.rearrange("p (a b) -> p a b", a=16)
            P.add("dve", lambda e: e.tensor_tensor(out=v3(ang), in0=posb, in1=invb, op=ALU.mult),
                  reads=K("posf", "cst"), writes=K(ang))
            P.add("dve", lambda e: e.tensor_scalar(out=uu.ap, in0=ang.ap, scalar1=1.0 / TWO_PI, scalar2=None, op0=ALU.mult),
                  reads=K(ang), writes=K(uu))
            P.add("dve", lambda e: e.tensor_copy(out=ki.ap, in_=uu.ap), reads=K(uu), writes=K(ki))
            P.add("dve", lambda e: e.tensor_copy(out=kf.ap, in_=ki.ap), reads=K(ki), writes=K(kf))
            C1 = 6.28125
            C2 = TWO_PI - C1
            P.add("dve", lambda e: e.scalar_tensor_tensor(out=rr.ap, in0=kf.ap, scalar=-C1, in1=ang.ap, op0=ALU.mult, op1=ALU.add),
                  reads=K(kf, ang), writes=K(rr))
            P.add("dve", lambda e: e.scalar_tensor_tensor(out=rr.ap, in0=kf.ap, scalar=-C2, in1=rr.ap, op0=ALU.mult, op1=ALU.add),
                  reads=K(kf, rr), writes=K(rr))
            P.add("dve", lambda e: e.tensor_scalar(out=mm.ap, in0=rr.ap, scalar1=math.pi, scalar2=None, op0=ALU.is_gt),
                  reads=K(rr), writes=K(mm))
            P.add("dve", lambda e: e.scalar_tensor_tensor(out=rr.ap, in0=mm.ap, scalar=-TWO_PI, in1=rr.ap, op0=ALU.mult, op1=ALU.add),
                  reads=K(mm, rr), writes=K(rr))
            P.add("dve", lambda e: e.tensor_scalar(out=mm.ap, in0=rr.ap, scalar1=-math.pi, scalar2=None, op0=ALU.is_lt),
                  reads=K(rr), writes=K(mm))
            P.add("dve", lambda e: e.scalar_tensor_tensor(out=rr.ap, in0=mm.ap, scalar=TWO_PI, in1=rr.ap, op0=ALU.mult, op1=ALU.add),
                  reads=K(mm, rr), writes=K(rr))
            P.add("dve", lambda e: e.tensor_scalar(out=rc.ap, in0=rr.ap, scalar1=math.pi / 2, scalar2=None, op0=ALU.add),
                  reads=K(rr), writes=K(rc))
            P.add("dve", lambda e: e.tensor_scalar(out=mm.ap, in0=rc.ap, scalar1=math.pi, scalar2=None, op0=ALU.is_gt),
                  reads=K(rc), writes=K(mm))
            P.add("dve", lambda e: e.scalar_tensor_tensor(out=rc.ap, in0=mm.ap, scalar=-TWO_PI, in1=rc.ap, op0=ALU.mult, op1=ALU.add),
                  reads=K(mm, rc), writes=K(rc))
            PIS = 3.1415925
            P.add("dve", lambda e: e.tensor_scalar(out=rr.ap, in0=rr.ap, scalar1=-PIS, scalar2=PIS, op0=ALU.max, op1=ALU.min),
                  reads=K(rr), writes=K(rr))
            P.add("dve", lambda e: e.tensor_scalar(out=rc.ap, in0=rc.ap, scalar1=-PIS, scalar2=PIS, op0=ALU.max, op1=ALU.min),
                  reads=K(rc), writes=K(rc))
            P.add("act", lambda e: e.activation(out=sin_t.ap, in_=rr.ap, func=AF.Sin), reads=K(rr), writes=K(sin_t))
            P.add("act", lambda e: e.activation(out=cs_t[:].rearrange("p a b -> p (a b)"), in_=rc.ap, func=AF.Sin),
                  reads=K(rc), writes=["cs_t"])
            P.add("dve", lambda e: e.tensor_scalar(out=sn_t[:, :, 0:16], in0=v3(sin_t)[:, :, 0:16], scalar1=-1.0, scalar2=None, op0=ALU.mult),
                  reads=K(sin_t), writes=["sn_t"])
            P.add("dve", lambda e: e.tensor_copy(out=sn_t[:, :, 16:32], in_=v3(sin_t)[:, :, 16:32]),
                  reads=K(sin_t), writes=["sn_t"])
            lam_src = prm[0]["lam"].rearrange("(o r) n -> o r n", o=1).broadcast_to([128, 4, 128])
            P.add("sp", lambda e: e.dma_start(out=lamv.ap, in_=lam_src), writes=K(lamv), dma="lamv")
            lv = lamv.ap.rearrange("p (a b) n -> p a b n", a=2)
            P.add("dve", lambda e: e.tensor_tensor(out=lprod.ap, in0=lv[:, :, 0, :], in1=lv[:, :, 1, :], op=ALU.mult),
                  reads=K(lamv), writes=K(lprod))
            P.add("dve", lambda e: e.tensor_reduce(out=lsum[:], in_=lprod.ap, axis=AX.X, op=ALU.add),
                  reads=K(lprod), writes=["lsum"])
            P.add("act", lambda e: e.activation(out=lexp[:], in_=lsum[:], func=AF.Exp), reads=["lsum"], writes=["lexp"])
            P.add("dve", lambda e: e.tensor_tensor(out=ldif[:], in0=lexp[:, 0:1], in1=lexp[:, 1:2], op=ALU.subtract),
                  reads=["lexp"], writes=["ldif"])
            lambda_init = 0.8 - 0.6 * math.exp(-0.3 * 0)
            P.add("dve", lambda e: e.tensor_scalar(out=nlam[:], in0=ldif[:], scalar1=-1.0, scalar2=-lambda_init, op0=ALU.mult, op1=ALU.add),
                  reads=["ldif"], writes=["nlam"])
            P.add("sp", lambda e: e.dma_start(out=subg[:], in_=prm[0]["sub"].broadcast_to([128, 256])), writes=["subg"], dma="subg")
            P.add("dve", lambda e: e.tensor_scalar(out=subg[:], in0=subg[:], scalar1=0.5 * (1.0 - lambda_init), scalar2=None, op0=ALU.mult),
                  reads=["subg"], writes=["subg"])

        if 1 in layers:
            wf_sb = sb("wf_sb", [128, 16, 16], BF16)
            nfb = sb("nfb", [16, 1], F32)
            wfv = prm[1]["w_in"].rearrange("(c p) n -> p c n", p=128)
            P.add("pool", lambda e, wfv=wfv: e.dma_start(out=wf_sb[:], in_=wfv[:, :, 8192:8208]), writes=["wf_sb"], dma="wf_sb")
            P.add("sp", lambda e: e.dma_start(out=nfb[:], in_=prm[1]["fb"].rearrange("o h -> h o")), writes=["nfb"], dma="nfb")
            P.add("dve", lambda e: e.tensor_scalar(out=nfb[:], in0=nfb[:], scalar1=-1.0, scalar2=None, op0=ALU.mult),
                  reads=["nfb"], writes=["nfb"])

        piece_ctr = [0]
        dmaq = []

        def pump(n=1):
            for _ in range(n):
                if dmaq:
                    dmaq.pop(0)[1]()

        def need_piece(pidx_):
            while dmaq and dmaq[0][0] <= pidx_:
                dmaq.pop(0)[1]()

        def queue_piece(w_in_d, blocks, slot, pidx_):
            wv = w_in_d.rearrange("(c p) n -> p c n", p=128)
            i = 0
            for (c0, w_, d0) in blocks:
                for cq in range(4):
                    def mk(i=i, c0=c0, w_=w_, d0=d0, cq=cq):
                        def rec():
                            P.add("pool", lambda e: e.dma_start(out=Wslot[slot][:, 4 * cq:4 * cq + 4, d0:d0 + w_],
                                                                 in_=wv[:, 4 * cq:4 * cq + 4, c0:c0 + w_]),
                                  writes=[("A1", slot, i)], dma=("w", slot))
                        return rec
                    dmaq.append((pidx_, mk()))
                    i += 1
            assert i == 8

        for li, l in enumerate(layers):
            P.new_epoch()
            pr = prm[l]
            is_fox = (l == 1)
            x_src = x_d if li == 0 else X1_d
            dst = out_d if li == len(layers) - 1 else X1_d

            g = gqk[l]
            qsrc = pr["qn"].rearrange("(o r) n -> o r n", o=1).broadcast_to([128, 2, 128])
            ksrc = pr["kn"].rearrange("(o r) n -> o r n", o=1).broadcast_to([128, 2, 128])
            P.add("sp", lambda e, g=g, qsrc=qsrc, ksrc=ksrc: [e.dma_start(out=g[:, 0:2, :], in_=qsrc), e.dma_start(out=g[:, 2:4, :], in_=ksrc)],
                  writes=[("gqk", l)], dma=("gqk", l), ninst=2)
            P.add("dve", lambda e, g=g: e.tensor_scalar(out=g[:, 0:2, :], in0=g[:, 0:2, :], scalar1=128.0 ** -0.5, scalar2=None, op0=ALU.mult),
                  reads=[("gqk", l)], writes=[("gqk", l)])

            A2.reset()
            xt = [A2.alloc([2048], F32) for _ in range(3)]
            xn = [A2.alloc([2048], BF16) for _ in range(3)]
            gnorm = A2.alloc([2048], F32)
            sqj = A2.alloc([2048], BF16)
            if li == 0:
                P.add("sp", lambda e, pr=pr, gnorm=gnorm: e.dma_start(out=gnorm.ap, in_=pr["norm"].broadcast_to([128, D])),
                      writes=K(gnorm), dma="gnorm")

            def load_x(t, src=x_src, xt=xt):
                s_ = t % 3
                P.add("sp", lambda e: e.dma_start(out=xt[s_].ap, in_=src[t * 128:(t + 1) * 128, :]),
                      reads=[("xsrc", id(src), t)], writes=K(xt[s_]), dma=("xt", s_))

            w_in_d = pr["w_in"]
            pieces = []
            for gi in range(8):
                pieces.append(("qk", gi, [(256 * gi, 256, 0), (2048 + 256 * gi, 256, 256)]))
                pieces.append(("vg", gi, [(4096 + 256 * gi, 256, 0), (6144 + 256 * gi, 256, 256)]))
            for pi in range(3):
                queue_piece(w_in_d, pieces[pi][2], pi % 3, pi)
            next_piece = [3]

            def n_stage_a(t, xt=xt, xn=xn, gnorm=gnorm, sqj=sqj):
                s_ = t % 3
                P.add("act", lambda e: e.activation(out=sqj.ap, in_=xt[s_].ap, func=AF.Square, accum_out=ssq[:, t:t + 1]),
                      reads=K(xt[s_]), writes=K(sqj, ("ssq", t)))
                P.add("dve", lambda e: e.tensor_scalar(out=var[:, t:t + 1], in0=ssq[:, t:t + 1], scalar1=1.0 / D, scalar2=EPS, op0=ALU.mult, op1=ALU.add),
                      reads=[("ssq", t)], writes=[("var", t)])
                P.add("pool", lambda e: e.tensor_tensor(out=rstd[:, t:t + 1], in0=var[:, t:t + 1], in1=cneg[:, 0:1], op=ALU.pow),
                      reads=[("var", t), "cneg"], writes=[("rstd", t)])
                P.add("dve", lambda e: e.scalar_tensor_tensor(out=xn[s_].ap, in0=xt[s_].ap, scalar=rstd[:, t:t + 1], in1=gnorm.ap,
                                                              op0=ALU.mult, op1=ALU.mult),
                      reads=K(xt[s_], ("rstd", t), gnorm), writes=K(xn[s_]))
                if t + 3 < NT:
                    load_x(t + 3)

            def n_stage_b(t, xn=xn):
                s_ = t % 3
                for half in range(2):
                    for j in range(8):
                        c = half * 8 + j
                        P.add("pe", lambda e, c=c, j=j, half=half: e.transpose(out=ptr[half][:, j * 128:(j + 1) * 128],
                                                                                in_=xn[s_].ap[:, c * 128:(c + 1) * 128], identity=ident[:]),
                              reads=K(xn[s_], "ident"), writes=[("ptr", half)])
                    P.add("act", lambda e, half=half: e.activation(out=hT[:, half * 8:half * 8 + 8, t * 128:(t + 1) * 128],
                                                                   in_=ptr[half][:].rearrange("p (a b) -> p a b", a=8), func=AF.Copy),
                          reads=[("ptr", half)], writes=[("hT", t)])

            if li == 0:
                for t in range(3):
                    load_x(t)
                n_stage_a(0)
                n_stage_a(1)
                for t in range(NT):
                    n_stage_b(t)
                    if t + 2 < NT:
                        n_stage_a(t + 2)
                    pump(2)
            while dmaq:
                pump()
            HT_ALL = [("hT", t) for t in range(NT)]

            if is_fox:
                A2.reset()
                ee = A2.alloc([2048], F32, parts=16)
                lf = ee
                ones16 = A2.alloc([2048], F32, parts=16)
                CC = A2.alloc([2048], F32, parts=16)
                r1 = A2.alloc([2048], F32, parts=16)
                hp = [A2.alloc([2048], BF16, parts=16) for _ in range(3)]
                np_ = [A2.alloc([2048], BF16, parts=16) for _ in range(3)]
                P.add("dve", lambda e: e.memset(ones16.ap, 1.0), writes=K(ones16))
                for tg in range(4):
                    for c in range(NCH):
                        P.add("pe", lambda e, tg=tg, c=c: e.matmul(acc[tg][0:16, :], lhsT=wf_sb[:, c, :], rhs=hT[:, c, tg * 512:(tg + 1) * 512],
                                                                   start=(c == 0), stop=(c == NCH - 1)),
                              reads=K("wf_sb", [("hT", t) for t in range(4 * tg, 4 * tg + 4)]), writes=[("acc", tg)])
                    P.add("act", lambda e, tg=tg: e.activation(out=ee.ap[:, tg * 512:(tg + 1) * 512], in_=acc[tg][0:16, :], func=AF.Exp,
                                                               bias=nfb[:, 0:1], scale=-1.0),
                          reads=[("acc", tg), "nfb"], writes=K(ee))
                P.add("act", lambda e: e.activation(out=lf.ap, in_=ee.ap, func=AF.Ln, bias=1.0, scale=1.0), reads=K(ee), writes=K(lf))
                P.add("dve", lambda e: e.tensor_tensor_scan(out=CC.ap, data0=ones16.ap, data1=lf.ap, initial=0.0, op0=ALU.mult, op1=ALU.add),
                      reads=K(ones16, lf), writes=K(CC))
                P.add("dve", lambda e: e.tensor_copy(out=hp[0].ap, in_=CC.ap), reads=K(CC), writes=K(hp[0]))
                P.add("dve", lambda e: e.tensor_tensor(out=r1.ap, in0=CC.ap, in1=hp[0].ap, op=ALU.subtract), reads=K(CC, hp[0]), writes=K(r1))
                P.add("dve", lambda e: e.tensor_copy(out=hp[1].ap, in_=r1.ap), reads=K(r1), writes=K(hp[1]))
                P.add("dve", lambda e: e.tensor_tensor(out=CC.ap, in0=r1.ap, in1=hp[1].ap, op=ALU.subtract), reads=K(r1, hp[1]), writes=K(CC))
                P.add("dve", lambda e: e.tensor_copy(out=hp[2].ap, in_=CC.ap), reads=K(CC), writes=K(hp[2]))
                for i in range(3):
                    P.add("dve", lambda e, i=i: e.tensor_scalar(out=np_[i].ap, in0=hp[i].ap, scalar1=-1.0, scalar2=None, op0=ALU.mult),
                          reads=K(hp[i]), writes=K(np_[i]))
                for i in range(3):
                    P.add("sp", lambda e, i=i: e.dma_start(out=AUG_d[i, :, :], in_=np_[i].ap), reads=K(np_[i]), writes=["AUG_d"], dma=("augw", i))
                    P.add("sp", lambda e, i=i: e.dma_start(out=AUG_d[3 + i, :, :], in_=hp[i].ap), reads=K(hp[i]), writes=["AUG_d"], dma=("augw", 3 + i))

            A2.reset()
            if is_fox:
                v_sb = A2.alloc([16, 2, 130], BF16)
            else:
                v_sb = A2.alloc([16, 260], BF16)
            gate_sb = A2.alloc([16, 256], BF16)
            oTg = [A2.alloc([2, 512], BF16) for _ in range(2)]
            og = [A2.alloc([4, 256], BF16) for _ in range(2)]
            PT = [A2.alloc([512], BF16) for _ in range(4)]
            sqs = [A2.alloc([512], F32) for _ in range(2)]
            sq = sqs[0]
            raw = [A2.alloc([4, 128], F32) for _ in range(3)]
            NQB = 6
            qkb = [A2.alloc([4, 128], BF16) for _ in range(NQB)]
            th = [A2.alloc([256], F32) for _ in range(2)]
            if is_fox:
                augq = [A2.alloc([2048], BF16) for _ in range(2)]
                augk = [A2.alloc([2048], BF16) for _ in range(2)]
            else:
                sg = [A2.alloc([256], F32) for _ in range(2)]
                gcp = [A2.alloc([256], F32) for _ in range(2)]
                t1 = A2.alloc([4, 256], F32)
                otmp = [A2.alloc([256], F32) for _ in range(4)]
            P.add("dve", lambda e, v_sb=v_sb: e.memset(v_sb.ap, 1.0), writes=K(v_sb))
            if is_fox:
                for i in range(2):
                    for av in (augq[i], augk[i]):
                        P.add("dve", lambda e, av=av: e.memset(av.ap, 0.0), writes=K(av))
                        P.add("dve", lambda e, av=av: e.memset(av.ap[0:6, :], 1.0), writes=K(av))

            evac_ctr = [0]
            pt_ctr = [0]
            sbanks3 = [(psA[0], ("psA", 0)), (psA[1], ("psA", 1)), (ptrf[1], ("ptr", 1))]
            og_pending = []
            evac2_pending = []
            wov = pr["w_out"].rearrange("(c p) n -> p c n", p=128)

            def load_wout(j, wov=wov):
                wk = wkeys(j) if j < 3 else [QKT_KEY]
                P.add("pool", lambda e: e.dma_start(out=Wout[:, 4 * j:4 * j + 4, :], in_=wov[:, 4 * j:4 * j + 4, :]),
                      writes=wk, dma=("wout", j))
            for gi in range(8):
                pidx = 2 * gi
                slot = pidx % 3
                need_piece(pidx)
                DEFER = 4

                def qk_transposes(t):
                    qb = qkb[t % NQB]
                    s2 = t % 2
                    for j in range(4):
                        P.add("pe", lambda e, j=j: e.matmul(ptrf[s2][:, j * 128:(j + 1) * 128], lhsT=qb.ap[:, j, :], rhs=ident[:], start=True, stop=True),
                              reads=K(qb, "ident"), writes=[("ptr", s2)])
                    P.add("act", lambda e: e.activation(out=qkT[:, :, t * 128:(t + 1) * 128],
                                                        in_=ptrf[s2][:].rearrange("p (a b) -> p a b", a=4), func=AF.Copy),
                          reads=[("ptr", s2)], writes=[QKT_KEY])

                def qk_stage1(t):
                    a = psA[t % 2]
                    akey = ("psA", t % 2)
                    s4 = t % 4
                    rw = raw[t % 3]
                    sqv = sqs[t % 2]
                    P.add("act", lambda e: e.activation(out=sqv.ap, in_=a[:], func=AF.Square), reads=[akey], writes=K(sqv))
                    P.add("act", lambda e: e.activation(out=rw.ap.rearrange("p a b -> p (a b)"), in_=a[:], func=AF.Copy), reads=[akey], writes=K(rw))
                    P.add("dve", lambda e: e.tensor_reduce(out=ssq4[:, s4, :], in_=sqv.ap.rearrange("p (a b) -> p a b", a=4), axis=AX.X, op=ALU.add),
                          reads=K(sqv), writes=[("ssq4", s4)])
                    P.add("dve", lambda e: e.tensor_scalar(out=var4[:, s4, :], in0=ssq4[:, s4, :], scalar1=1.0 / 128, scalar2=EPS, op0=ALU.mult, op1=ALU.add),
                          reads=[("ssq4", s4)], writes=[("var4", s4)])
                    P.add("pool", lambda e: e.tensor_tensor(out=rstd4[:, s4, :], in0=var4[:, s4, :], in1=cneg[:, 0:4], op=ALU.pow),
                          reads=[("var4", s4), "cneg"], writes=[("rstd4", s4)])

                def qk_stage2(t):
                    s2 = t % 2
                    s4 = t % 4
                    rw = raw[t % 3]
                    qb = qkb[t % NQB]
                    if not is_fox:
                        rb32 = bass.AP(rstd4, s4 * 4, [[16, 128], [1, 4], [0, 32]])
                        P.add("pool", lambda e: e.tensor_tensor(out=qr[s2][:], in0=rw.ap[:, :, 0:32], in1=rb32, op=ALU.mult),
                              reads=K(rw, ("rstd4", s4)), writes=[("qr", s2)])
                        P.add("pool", lambda e: e.tensor_tensor(out=qr[s2][:], in0=qr[s2][:], in1=g[:, :, 0:32], op=ALU.mult),
                              reads=[("qr", s2), ("gqk", l)], writes=[("qr", s2)])
                    for h in range(4):
                        P.add("dve", lambda e, h=h: e.scalar_tensor_tensor(out=qb.ap[:, h, :], in0=rw.ap[:, h, :], scalar=rstd4[:, s4, h:h + 1], in1=g[:, h, :],
                                                                           op0=ALU.mult, op1=ALU.mult),
                              reads=K(rw, ("rstd4", s4), ("gqk", l)), writes=K(qb))
                    if not is_fox:
                        csb = bass.AP(cs_t, t * 32, [[512, 128], [0, 4], [1, 32]])
                        snb = bass.AP(sn_t, t * 32, [[512, 128], [0, 4], [16, 2], [1, 16]])
                        qsw = bass.AP(qr[s2], 16, [[128, 128], [32, 4], [-16, 2], [1, 16]])
                        P.add("dve", lambda e: e.tensor_tensor(out=ra[s2][:], in0=qr[s2][:], in1=csb, op=ALU.mult),
                              reads=[("qr", s2), "cs_t"], writes=[("ra", s2)])
                        P.add("dve", lambda e: e.tensor_tensor(out=rb[s2][:].rearrange("p h (a b) -> p h a b", a=2), in0=qsw, in1=snb, op=ALU.mult),
                              reads=[("qr", s2), "sn_t"], writes=[("rb", s2)])
                        P.add("dve", lambda e: e.tensor_tensor(out=qb.ap[:, :, 0:32], in0=ra[s2][:], in1=rb[s2][:], op=ALU.add),
                              reads=[("ra", s2), ("rb", s2)] + K(qb), writes=K(qb))

                for t in range(NT):
                    a = psA[t % 2]
                    akey = ("psA", t % 2)
                    for c in range(NCH):
                        P.add("pe", lambda e, a=a, c=c, t=t, slot=slot: e.matmul(a[:], lhsT=hT[:, c, t * 128:(t + 1) * 128], rhs=Wslot[slot][:, c, :],
                                                                                  start=(c == 0), stop=(c == NCH - 1)),
                              reads=K(("hT", t), wkeys(slot)), writes=[akey])
                    qk_stage1(t)
                    if t >= 1:
                        qk_stage2(t - 1)
                    if t >= DEFER:
                        qk_transposes(t - DEFER)
                    if t == 3 and og_pending:
                        og_pending.pop(0)()
                    if t % 2 == 1:
                        pump(1)
                qk_stage2(NT - 1)
                qk_pending = list(range(NT - DEFER, NT))
                if next_piece[0] < len(pieces):
                    queue_piece(w_in_d, pieces[next_piece[0]][2], next_piece[0] % 3, next_piece[0])
                    next_piece[0] += 1
                pidx = 2 * gi + 1
                slot = pidx % 3
                need_piece(pidx)
                for t in range(NT):
                    a = psA[t % 2]
                    akey = ("psA", t % 2)
                    s2 = t % 2
                    for c in range(NCH):
                        P.add("pe", lambda e, a=a, c=c, t=t, slot=slot: e.matmul(a[:], lhsT=hT[:, c, t * 128:(t + 1) * 128], rhs=Wslot[slot][:, c, :],
                                                                                  start=(c == 0), stop=(c == NCH - 1)),
                              reads=K(("hT", t), wkeys(slot)), writes=[akey])
                    if is_fox:
                        P.add("act", lambda e, a=a, t=t: e.activation(out=v_sb.ap[:, t, :, 0:128], in_=a[:, 0:256].rearrange("p (h d) -> p h d", h=2),
                                                                      func=AF.Copy, scale=0.5),
                              reads=[akey], writes=K(v_sb))
                    else:
                        P.add("act", lambda e, a=a, t=t: e.activation(out=v_sb.ap[:, t, 0:256], in_=a[:, 0:256], func=AF.Copy),
                              reads=[akey], writes=K(v_sb))
                    P.add("act", lambda e, a=a, s2=s2: e.activation(out=th[s2].ap, in_=a[:, 256:512], func=AF.Tanh, scale=0.5),
                          reads=[akey], writes=K(th[s2]))
                    if qk_pending and t >= 4:
                        qk_transposes(qk_pending.pop(0))
                    if is_fox:
                        P.add("dve", lambda e, a=a, s2=s2, t=t: e.scalar_tensor_tensor(out=gate_sb.ap[:, t, :], in0=th[s2].ap, scalar=1.0, in1=a[:, 256:512],
                                                                                         op0=ALU.add, op1=ALU.mult),
                              reads=K(th[s2], akey), writes=K(gate_sb))
                    else:
                        P.add("act", lambda e, a=a, s2=s2: e.activation(out=gcp[s2].ap, in_=a[:, 256:512], func=AF.Copy),
                              reads=[akey], writes=K(gcp[s2]))
                        P.add("dve", lambda e, s2=s2: e.scalar_tensor_tensor(out=sg[s2].ap, in0=th[s2].ap, scalar=1.0, in1=gcp[s2].ap,
                                                                              op0=ALU.add, op1=ALU.mult),
                              reads=K(th[s2], gcp[s2]), writes=K(sg[s2]))
                        P.add("pool", lambda e, s2=s2, t=t: e.tensor_tensor(out=gate_sb.ap[:, t, :], in0=sg[s2].ap, in1=subg[:], op=ALU.mult),
                              reads=K(sg[s2], "subg"), writes=K(gate_sb))
                    if t % 2 == 1:
                        pump(1)
                if next_piece[0] < len(pieces):
                    queue_piece(w_in_d, pieces[next_piece[0]][2], next_piece[0] % 3, next_piece[0])
                    next_piece[0] += 1
                if gi == 7:
                    assert not dmaq
                    for j in range(3):
                        load_wout(j)

                for qg in range(4):
                    ogs = og[qg % 2]
                    for u in range(2):
                        if is_fox:
                            H = 2 * gi + u
                            ab = (2 * gi + u) % 2
                            P.add("sp", lambda e, ab=ab, H=H: e.dma_start(out=augq[ab].ap[0:3, :], in_=AUG_d[0:3, H, :]),
                                  reads=["AUG_d"], writes=K(augq[ab]), dma=("augq", ab)) if qg == 0 else None
                            P.add("sp", lambda e, ab=ab, H=H: e.dma_start(out=augk[ab].ap[3:6, :], in_=AUG_d[3:6, H, :]),
                                  reads=["AUG_d"], writes=K(augk[ab]), dma=("augk", ab)) if qg == 0 else None
                            Nv = 129
                            vv = lambda kt, u=u, v_sb=v_sb: v_sb.ap[:, kt, u, 0:129]
                        else:
                            Nv = 257
                            vv = lambda kt, v_sb=v_sb: v_sb.ap[:, kt, 0:257]
                        steps = list(range(4 * qg + 4))

                        def rec_S(kt, u=u, qg=qg):
                            q0 = max(4 * qg, kt) * 128
                            n = (4 * qg + 4) * 128 - q0
                            b, bkey = sbanks3[kt % 3]
                            diag = kt >= 4 * qg
                            last_plain = (not is_fox) and (not diag)
                            P.add("pe", lambda e: e.matmul(b[:, 0:n], lhsT=qkT[:, 2 + u, kt * 128:(kt + 1) * 128], rhs=qkT[:, u, q0:q0 + n],
                                                           start=True, stop=last_plain),
                                  reads=[QKT_KEY], writes=[bkey])
                            if is_fox:
                                P.add("pe", lambda e: e.matmul(b[:, 0:n], lhsT=augk[ab].ap[:, kt * 128:(kt + 1) * 128], rhs=augq[ab].ap[:, q0:q0 + n],
                                                               start=False, stop=(not diag)),
                                      reads=K(augk[ab], augq[ab]), writes=[bkey])
                            if diag:
                                mk_ = maskF if is_fox else maskD
                                mkey = "maskF" if is_fox else "maskD"
                                P.add("pe", lambda e: e.matmul(b[:, 0:128], lhsT=ident[:], rhs=mk_[:], start=False, stop=True),
                                      reads=["ident", mkey], writes=[bkey])
                            return (b, bkey, q0, n)

                        def rec_rest(kt, info, u=u, qg=qg):
                            b, bkey, q0, n = info
                            pslot = pt_ctr[0] % 4
                            pt_ctr[0] += 1
                            ptile = PT[pslot]
                            P.add("act", lambda e: e.activation(out=ptile.ap[:, 0:n], in_=b[:, 0:n], func=AF.Exp),
                                  reads=[bkey], writes=K(ptile))
                            for qt in range(max(4 * qg, kt), 4 * qg + 4):
                                j = qt - 4 * qg
                                off = qt * 128 - q0
                                P.add("pe", lambda e, j=j, off=off, qt=qt: e.matmul(acc[j][:, 0:Nv], lhsT=ptile.ap[:, off:off + 128], rhs=vv(kt),
                                                                                     start=(kt == 0), stop=(kt == qt)),
                                      reads=K(ptile, v_sb), writes=[("acc", j)])

                        infos = {}
                        for si in range(min(2, len(steps))):
                            infos[si] = rec_S(steps[si])
                        for si, kt in enumerate(steps):
                            if si + 2 < len(steps):
                                infos[si + 2] = rec_S(steps[si + 2])
                            rec_rest(kt, infos.pop(si))

                        while evac2_pending:
                            evac2_pending.pop(0)()
                        esb = (evac_ctr[0] % 2) * 4
                        evac_ctr[0] += 1
                        for j in range(4):
                            qt = 4 * qg + j
                            es_ = esb + j
                            dcol = 128 if is_fox else 256
                            P.add("dve", lambda e, j=j, es_=es_, dcol=dcol: e.reciprocal(out=rinv[:, es_:es_ + 1], in_=acc[j][:, dcol:dcol + 1]),
                                  reads=[("acc", j)], writes=[("rinv", es_)])
                            if is_fox:
                                P.add("dve", lambda e, j=j, es_=es_, qt=qt, u=u, ogs=ogs: e.scalar_tensor_tensor(
                                    out=ogs.ap[:, j, u * 128:(u + 1) * 128], in0=acc[j][:, 0:128], scalar=rinv[:, es_:es_ + 1],
                                    in1=gate_sb.ap[:, qt, u * 128:(u + 1) * 128], op0=ALU.mult, op1=ALU.mult),
                                    reads=K(("acc", j), ("rinv", es_), gate_sb), writes=K(ogs))
                            elif u == 0:
                                P.add("dve", lambda e, j=j, es_=es_: e.tensor_scalar(out=t1.ap[:, j, :], in0=acc[j][:, 0:256], scalar1=rinv[:, es_:es_ + 1],
                                                                                      scalar2=None, op0=ALU.mult),
                                      reads=[("acc", j), ("rinv", es_)], writes=K(t1))
                            else:
                                o2 = otmp[j]
                                P.add("dve", lambda e, es_=es_: e.tensor_tensor(out=nl2[:, es_:es_ + 1], in0=rinv[:, es_:es_ + 1], in1=nlam[:], op=ALU.mult),
                                      reads=[("rinv", es_), "nlam"], writes=[("nl2", es_)])
                                P.add("dve", lambda e, j=j, es_=es_, o2=o2: e.scalar_tensor_tensor(out=o2.ap, in0=acc[j][:, 0:256], scalar=nl2[:, es_:es_ + 1],
                                                                                                     in1=t1.ap[:, j, :], op0=ALU.mult, op1=ALU.add),
                                      reads=K(("acc", j), ("nl2", es_), t1), writes=K(o2))
                        if (not is_fox) and u == 1:
                            for j in range(4):
                                es_ = esb + j
                                o2 = otmp[j]
                                jk = th[j % 2]
                                P.add("act", lambda e, es_=es_, o2=o2, jk=jk: e.activation(out=jk.ap, in_=o2.ap, func=AF.Square, accum_out=sso[:, es_:es_ + 1]),
                                      reads=K(o2), writes=K(jk, ("sso", es_)))

                            def evac2(esb=esb, qg=qg, ogs=ogs):
                                P.add("dve", lambda e: e.tensor_scalar(out=varo[:, esb:esb + 4], in0=sso[:, esb:esb + 4], scalar1=1.0 / 256, scalar2=EPS,
                                                                       op0=ALU.mult, op1=ALU.add),
                                      reads=[("sso", esb + j) for j in range(4)], writes=[("varo", esb + j) for j in range(4)])
                                P.add("pool", lambda e: e.tensor_tensor(out=rso[:, esb:esb + 4], in0=varo[:, esb:esb + 4], in1=cneg[:, 0:4], op=ALU.pow),
                                      reads=[("varo", esb + j) for j in range(4)] + ["cneg"], writes=[("rso", esb + j) for j in range(4)])
                                for j in range(4):
                                    qt = 4 * qg + j
                                    es_ = esb + j
                                    o2 = otmp[j]
                                    P.add("dve", lambda e, j=j, es_=es_, o2=o2, qt=qt: e.scalar_tensor_tensor(
                                        out=ogs.ap[:, j, :], in0=o2.ap, scalar=rso[:, es_:es_ + 1], in1=gate_sb.ap[:, qt, :], op0=ALU.mult, op1=ALU.mult),
                                        reads=K(o2, ("rso", es_), gate_sb), writes=K(ogs))
                            evac2_pending.append(evac2)
                    def og_tr(qg=qg, ogs=ogs, gi=gi, l=l, oTg=oTg):
                        tp = ptr[0]
                        for c in range(2):
                            for j in range(4):
                                P.add("pe", lambda e, c=c, j=j: e.transpose(out=tp[:, (c * 4 + j) * 128:(c * 4 + j + 1) * 128],
                                                                            in_=ogs.ap[:, j, c * 128:(c + 1) * 128], identity=ident[:]),
                                      reads=K(ogs, "ident"), writes=[("ptr", 0)])
                        ot_ = oTg[qg % 2]
                        P.add("act", lambda e: e.activation(out=ot_.ap, in_=tp[:].rearrange("p (c n) -> p c n", c=2), func=AF.Copy),
                              reads=[("ptr", 0)], writes=K(ot_))
                        P.add("sp", lambda e: e.dma_start(out=OT_d[l][gi * 256:(gi + 1) * 256, qg * 512:(qg + 1) * 512].rearrange("(c p) t -> p c t", p=128),
                                                          in_=ot_.ap),
                              reads=K(ot_), writes=[("OT", l, qg)], dma=("oTg", qg % 2))
                    if og_pending:
                        og_pending.pop(0)()
                    og_pending.append(og_tr)
                    pump(1)
                while evac2_pending:
                    evac2_pending.pop(0)()
            while og_pending:
                og_pending.pop(0)()
            while dmaq:
                pump()

            load_wout(3)
            fuse_next = (li + 1 < len(layers))
            A2.reset()
            xt = [A2.alloc([2048], F32) for _ in range(3)]
            oTin = [A2.alloc([16, 256], BF16) for _ in range(2)]
            if fuse_next:
                xn = [A2.alloc([2048], BF16) for _ in range(2)]
                gnorm = A2.alloc([2048], F32)
                sqj = A2.alloc([2048], BF16)
                nprm = prm[layers[li + 1]]
                P.add("sp", lambda e, nprm=nprm, gnorm=gnorm: e.dma_start(out=gnorm.ap, in_=nprm["norm"].broadcast_to([128, D])),
                      writes=K(gnorm), dma="gnorm")

                def o_stage_a(t, xt=xt, xn=xn, gnorm=gnorm, sqj=sqj):
                    s_ = t % 3
                    s2 = t % 2
                    P.add("act", lambda e: e.activation(out=sqj.ap, in_=xt[s_].ap, func=AF.Square, accum_out=ssq[:, t:t + 1]),
                          reads=K(xt[s_]), writes=K(sqj, ("ssq", t)))
                    P.add("dve", lambda e: e.tensor_scalar(out=var[:, t:t + 1], in0=ssq[:, t:t + 1], scalar1=1.0 / D, scalar2=EPS, op0=ALU.mult, op1=ALU.add),
                          reads=[("ssq", t)], writes=[("var", t)])
                    P.add("pool", lambda e: e.tensor_tensor(out=rstd[:, t:t + 1], in0=var[:, t:t + 1], in1=cneg[:, 0:1], op=ALU.pow),
                          reads=[("var", t), "cneg"], writes=[("rstd", t)])
                    P.add("dve", lambda e: e.scalar_tensor_tensor(out=xn[s2].ap, in0=xt[s_].ap, scalar=rstd[:, t:t + 1], in1=gnorm.ap,
                                                                  op0=ALU.mult, op1=ALU.mult),
                          reads=K(xt[s_], ("rstd", t), gnorm), writes=K(xn[s2]))

                def o_stage_b(t, xn=xn):
                    s2 = t % 2
                    for half in range(2):
                        for j in range(8):
                            c = half * 8 + j
                            P.add("pe", lambda e, c=c, j=j, half=half: e.transpose(out=ptr[half][:, j * 128:(j + 1) * 128],
                                                                                    in_=xn[s2].ap[:, c * 128:(c + 1) * 128], identity=ident[:]),
                                  reads=K(xn[s2], "ident"), writes=[("ptr", half)])
                        P.add("act", lambda e, half=half: e.activation(out=hT[:, half * 8:half * 8 + 8, t * 128:(t + 1) * 128],
                                                                       in_=ptr[half][:].rearrange("p (a b) -> p a b", a=8), func=AF.Copy),
                              reads=[("ptr", half)], writes=[("hT", t)])
            otv = OT_d[l].rearrange("(c p) t -> p c t", p=128)

            def load_o(tq, oTin=oTin, otv=otv):
                s_ = tq % 2
                P.add("sp", lambda e: [e.dma_start(out=oTin[s_].ap[:, 8 * i:8 * i + 8, :], in_=otv[:, 8 * i:8 * i + 8, tq * 256:(tq + 1) * 256]) for i in range(2)],
                      reads=[("OT", l, tq // 2)], writes=K(oTin[s_]), dma=("oTin", s_), ninst=2)

            def load_xr(t, src=x_src, xt=xt):
                s_ = t % 3
                P.add("sp", lambda e: e.dma_start(out=xt[s_].ap, in_=src[t * 128:(t + 1) * 128, :]),
                      reads=[("xsrc", id(src), t)], writes=K(xt[s_]), dma=("xt", s_))

            load_o(0)
            load_xr(0)
            load_xr(1)
            load_o(1)
            for t in range(NT):
                tq, tt = divmod(t, 2)
                s_ = t % 3
                so = tq % 2
                for n in range(4):
                    y = psA[n % 2]
                    ykey = ("psA", n % 2)
                    for c in range(NCH):
                        P.add("pe", lambda e, y=y, c=c, so=so, tt=tt, n=n: e.matmul(y[:], lhsT=oTin[so].ap[:, c, tt * 128:(tt + 1) * 128],
                                                                                    rhs=Wout[:, c, n * 512:(n + 1) * 512], start=(c == 0), stop=(c == NCH - 1)),
                              reads=K(oTin[so], A1_ALL), writes=[ykey])
                    P.add("dve", lambda e, y=y, s_=s_, n=n: e.tensor_tensor(out=xt[s_].ap[:, n * 512:(n + 1) * 512], in0=y[:],
                                                                            in1=xt[s_].ap[:, n * 512:(n + 1) * 512], op=ALU.add),
                          reads=K(ykey, xt[s_]), writes=K(xt[s_]))
                if t + 2 < NT:
                    load_xr(t + 2)
                if tt == 1 and tq + 2 < 8:
                    load_o(tq + 2)
                P.add("sp", lambda e, t=t, s_=s_, dst=dst: e.dma_start(out=dst[t * 128:(t + 1) * 128, :], in_=xt[s_].ap),
                      reads=K(xt[s_]), writes=[("xsrc", id(dst), t)], dma=("xst", s_))
                if fuse_next:
                    o_stage_a(t)
                    if t >= 1:
                        o_stage_b(t - 1)
            if fuse_next:
                o_stage_b(NT - 1)

        P.emit(nc)
    STATS[tuple(layers)] = dict(nops=len(P.ops), nsems=P.n_sems, maxcount=P.max_count, nwaits=P.nwaits)
    return nc


def _consts():
    c = np.zeros((128, 416), np.float32)
    c[:, 0:128] = np.eye(128, dtype=np.float32)
    k = np.arange(128)[:, None]
    q = np.arange(128)[None, :]
    c[:, 128:256] = np.where(k > q, NEG, 0.0)
    c[:, 256:384] = np.where((k // 64) > (q // 64), NEG, 0.0)
    half = 16
    inv = np.float32(500000.0) ** (-(np.arange(half, dtype=np.float32) / np.float32(half)))
    c[:, 384:400] = inv[None, :]
    c[:, 400:416] = inv[None, :]
    return c


_NC_CACHE = {}


def _get_nc(layers):
    if layers not in _NC_CACHE:
        _NC_CACHE[layers] = build(layers)
    return _NC_CACHE[layers]


def _maps(layers, xs, positions, prm):
    cst = _consts()
    maps = []
    for b in range(8):
        m = {"x": np.ascontiguousarray(xs[b]), "cst": cst}
        if 0 in layers:
            m["pos"] = np.ascontiguousarray(positions[b].reshape(16, 128).T.astype(np.int32))
            m["a_norm"] = prm["a_norm"]
            m["a_w_in"] = prm["a_w_in"]
            m["a_q_norm"] = prm["a_q_norm"]
            m["a_k_norm"] = prm["a_k_norm"]
            m["a_lam"] = prm["a_lam"]
            m["a_sub_norm"] = prm["a_sub_norm"]
            m["a_w_out"] = prm["a_w_out"]
        if 1 in layers:
            m["b_norm"] = prm["b_norm"]
            m["b_w_in"] = prm["b_w_in"]
            m["b_f_bias"] = prm["b_f_bias"]
            m["b_q_norm"] = prm["b_q_norm"]
            m["b_k_norm"] = prm["b_k_norm"]
            m["b_w_out"] = prm["b_w_out"]
        maps.append(m)
    return maps


FUSED = True


def kernel(x, positions, a_norm, a_w_in, a_q_norm, a_k_norm, a_lambda_q1, a_lambda_k1,
           a_lambda_q2, a_lambda_k2, a_sub_norm, a_w_out, b_norm, b_w_in, b_f_bias,
           b_q_norm, b_k_norm, b_w_out):
    f = lambda a: np.ascontiguousarray(np.asarray(a, dtype=np.float32))
    prm = dict(
        a_norm=f(a_norm).reshape(1, D), a_w_in=f(a_w_in)[0], a_q_norm=f(a_q_norm).reshape(1, 128),
        a_k_norm=f(a_k_norm).reshape(1, 128),
        a_lam=np.ascontiguousarray(np.concatenate([f(a_lambda_q1).reshape(1, 128), f(a_lambda_k1).reshape(1, 128),
                                                   f(a_lambda_q2).reshape(1, 128), f(a_lambda_k2).reshape(1, 128)], axis=0)),
        a_sub_norm=f(a_sub_norm).reshape(1, 256), a_w_out=f(a_w_out)[0],
        b_norm=f(b_norm).reshape(1, D), b_w_in=f(b_w_in)[0], b_f_bias=f(b_f_bias).reshape(1, 16),
        b_q_norm=f(b_q_norm).reshape(1, 128), b_k_norm=f(b_k_norm).reshape(1, 128), b_w_out=f(b_w_out)[0],
    )
    x = f(x)
    positions = np.asarray(positions)
    cores = list(range(8))
    if FUSED:
        nc = _get_nc((0, 1))
        res = run_bass_kernel_spmd(nc, _maps((0, 1), x, positions, prm), core_ids=cores)
        return np.stack([r["out"] for r in res.results], axis=0)
    nc0 = _get_nc((0,))
    res0 = run_bass_kernel_spmd(nc0, _maps((0,), x, positions, prm), core_ids=cores)
    x1 = [r["out"] for r in res0.results]
    nc1 = _get_nc((1,))
    res1 = run_bass_kernel_spmd(nc1, _maps((1,), x1, positions, prm), core_ids=cores)
    return np.stack([r["out"] for r in res1.results], axis=0)
```
